# Optimizing a Trainium2 kernel written in Bass

```python
import math
import jax, jax.numpy as jnp
from jax import lax
import numpy as np

D_MODEL = 1024
BATCH = 4
SEQ = 4096
DEPTH = 4

D_MIX = D_MODEL
D_POOL = D_MIX // 2
D_SSM = D_MIX - D_POOL
POOL_WINDOWS = (2, 4, 8, 16)
N_POOL_GROUPS = len(POOL_WINDOWS)
POOL_GROUP = D_POOL // N_POOL_GROUPS
SSM_GROUP = 16
N_SSM_GROUPS = D_SSM // SSM_GROUP
SSM_STATE = 64
D_FF = -(-8 * D_MODEL // (3 * 256)) * 256
RMS_EPS = 1e-6
DT_MIN = 1e-3
DT_MAX = 1e-1

kernel_name = "hybrid_pool_s5_parallel_heads"


def rmsnorm(x, g):
    xf = x.astype(jnp.float32)
    y = xf * lax.rsqrt(jnp.mean(xf * xf, axis=-1, keepdims=True) + RMS_EPS)
    return y.astype(x.dtype) * g


def pool_mixer(u, w_pool, scale):
    b, l, _ = u.shape
    ug = u.astype(jnp.float32).reshape(b, l, N_POOL_GROUPS, POOL_GROUP)
    cs = jnp.cumsum(ug, axis=1)
    n_pos = jnp.arange(1, l + 1, dtype=jnp.float32)
    diffs = []
    for gi, w in enumerate(POOL_WINDOWS):
        c = cs[:, :, gi]
        lagged = jnp.pad(c, ((0, 0), (w, 0), (0, 0)))[:, :l]
        mean = (c - lagged) / jnp.minimum(n_pos, float(w))[None, :, None]
        diffs.append(mean - ug[:, :, gi])
    d = jnp.stack(diffs, axis=2)
    y = jnp.einsum('blgc,gcd->blgd', d, w_pool.astype(jnp.float32)).reshape(b, l, D_POOL)
    return (y * scale.astype(jnp.float32)).astype(u.dtype)


def _complex_affine_combine(e1, e2):
    a1r, a1i, b1r, b1i = e1
    a2r, a2i, b2r, b2i = e2
    ar = a2r * a1r - a2i * a1i
    ai = a2r * a1i + a2i * a1r
    br = a2r * b1r - a2i * b1i + b2r
    bi = a2r * b1i + a2i * b1r + b2i
    return ar, ai, br, bi


def ssm_mixer(u, lam_re, lam_im, log_dt, b_re, b_im, c_re, c_im, d_skip, w_glu, b_glu):
    f32 = jnp.float32
    bsz, l, _ = u.shape
    ug = u.astype(f32).reshape(bsz, l, N_SSM_GROUPS, SSM_GROUP)
    lr, li = lam_re.astype(f32), lam_im.astype(f32)
    dt = jnp.exp(log_dt.astype(f32))[:, None]
    mag = jnp.exp(lr * dt)
    abar_r = mag * jnp.cos(li * dt)
    abar_i = mag * jnp.sin(li * dt)
    den = lr * lr + li * li
    nr, ni = abar_r - 1.0, abar_i
    coef_r = (nr * lr + ni * li) / den
    coef_i = (ni * lr - nr * li) / den
    br, bi = b_re.astype(f32), b_im.astype(f32)
    bbar_r = coef_r[..., None] * br - coef_i[..., None] * bi
    bbar_i = coef_r[..., None] * bi + coef_i[..., None] * br
    bu_r = jnp.einsum('blgh,gph->blgp', ug, bbar_r)
    bu_i = jnp.einsum('blgh,gph->blgp', ug, bbar_i)
    a_r = jnp.broadcast_to(abar_r, (1, l, N_SSM_GROUPS, SSM_STATE))
    a_i = jnp.broadcast_to(abar_i, (1, l, N_SSM_GROUPS, SSM_STATE))
    _, _, s_r, s_i = lax.associative_scan(_complex_affine_combine, (a_r, a_i, bu_r, bu_i), axis=1)
    y = (jnp.einsum('blgp,ghp->blgh', s_r, c_re.astype(f32))
         - jnp.einsum('blgp,ghp->blgh', s_i, c_im.astype(f32))
         + d_skip.astype(f32) * ug).reshape(bsz, l, D_SSM)
    y = jax.nn.gelu(y)
    y = y * jax.nn.sigmoid(y @ w_glu.astype(f32) + b_glu.astype(f32))
    return y.astype(u.dtype)


def swiglu(h, w_gate, w_up, w_down):
    return (jax.nn.silu(h @ w_gate) * (h @ w_up)) @ w_down


def setup_inputs(seed: int = 0) -> dict:
    key = jax.random.key(seed)
    ks = jax.random.split(key, 24)
    f32 = jnp.float32
    nrm = lambda k, s, sc: jax.random.normal(k, s, f32) * sc
    res_scale = (2 * DEPTH) ** -0.5
    n_idx = jnp.arange(SSM_STATE, dtype=f32)
    lam_re = -0.5 + nrm(ks[5], (DEPTH, N_SSM_GROUPS, SSM_STATE), 0.01)
    lam_im = math.pi * n_idx[None, None, :] + nrm(ks[6], (DEPTH, N_SSM_GROUPS, SSM_STATE), 0.01)
    log_dt = jax.random.uniform(ks[7], (DEPTH, N_SSM_GROUPS), f32, math.log(DT_MIN), math.log(DT_MAX))
    return {
        "x": nrm(ks[0], (BATCH, SEQ, D_MODEL), 1.0),
        "norm_mix": 1.0 + nrm(ks[1], (DEPTH, D_MODEL), 0.02),
        "w_in": nrm(ks[2], (DEPTH, D_MODEL, D_MIX), D_MODEL ** -0.5),
        "w_pool": nrm(ks[3], (DEPTH, N_POOL_GROUPS, POOL_GROUP, POOL_GROUP), POOL_GROUP ** -0.5),
        "pool_scale": 1.0 + nrm(ks[4], (DEPTH, D_POOL), 0.02),
        "lam_re": lam_re,
        "lam_im": lam_im,
        "log_dt": log_dt,
        "b_re": nrm(ks[8], (DEPTH, N_SSM_GROUPS, SSM_STATE, SSM_GROUP), (2 * SSM_GROUP) ** -0.5),
        "b_im": nrm(ks[9], (DEPTH, N_SSM_GROUPS, SSM_STATE, SSM_GROUP), (2 * SSM_GROUP) ** -0.5),
        "c_re": nrm(ks[10], (DEPTH, N_SSM_GROUPS, SSM_GROUP, SSM_STATE), (2 * SSM_STATE) ** -0.5),
        "c_im": nrm(ks[11], (DEPTH, N_SSM_GROUPS, SSM_GROUP, SSM_STATE), (2 * SSM_STATE) ** -0.5),
        "d_skip": nrm(ks[12], (DEPTH, N_SSM_GROUPS, SSM_GROUP), 1.0),
        "w_glu": nrm(ks[13], (DEPTH, D_SSM, D_SSM), D_SSM ** -0.5),
        "b_glu": nrm(ks[14], (DEPTH, D_SSM), 0.01),
        "w_out": nrm(ks[15], (DEPTH, D_MIX, D_MODEL), D_MIX ** -0.5 * res_scale),
        "norm_ffn": 1.0 + nrm(ks[16], (DEPTH, D_MODEL), 0.02),
        "w_gate": nrm(ks[17], (DEPTH, D_MODEL, D_FF), D_MODEL ** -0.5),
        "w_up": nrm(ks[18], (DEPTH, D_MODEL, D_FF), D_MODEL ** -0.5),
        "w_down": nrm(ks[19], (DEPTH, D_FF, D_MODEL), D_FF ** -0.5 * res_scale),
        "norm_final": 1.0 + nrm(ks[20], (D_MODEL,), 0.02),
    }


def reference(x, norm_mix, w_in, w_pool, pool_scale, lam_re, lam_im, log_dt, b_re, b_im,
              c_re, c_im, d_skip, w_glu, b_glu, w_out, norm_ffn, w_gate, w_up, w_down,
              norm_final):
    h = x
    for i in range(DEPTH):
        u = rmsnorm(h, norm_mix[i]) @ w_in[i]
        u_pool, u_ssm = u[..., :D_POOL], u[..., D_POOL:]
        y_pool = pool_mixer(u_pool, w_pool[i], pool_scale[i])
        y_ssm = ssm_mixer(u_ssm, lam_re[i], lam_im[i], log_dt[i], b_re[i], b_im[i],
                          c_re[i], c_im[i], d_skip[i], w_glu[i], b_glu[i])
        h = h + jnp.concatenate([y_pool, y_ssm], axis=-1) @ w_out[i]
        h = h + swiglu(rmsnorm(h, norm_ffn[i]), w_gate[i], w_up[i], w_down[i])
    return rmsnorm(h, norm_final)
```

```python
import math
import numpy as np
import concourse.bass as bass
import concourse.mybir as mybir
from concourse.bass_utils import run_bass_kernel_spmd

F32 = mybir.dt.float32
BF16 = mybir.dt.bfloat16
ALU = mybir.AluOpType
AF = mybir.ActivationFunctionType

DEPTH = 4
D = 1024
DT = 8
NTOK = 2048
TT = 4
DFF = 2816
FT = 22
T1 = 16
NC1 = 128
NPAIR = 16
EPS = 1e-6
SBUF_BASE = 16512
SBUF_LIMIT = 229376
CARRY_W = 2 * NPAIR + 4 * 16
TWO_PI = 2.0 * math.pi


class Key:
    __slots__ = ("space", "lo", "hi", "name", "last_w", "readers", "ovl")

    def __init__(self, space, lo, hi, name):
        self.space, self.lo, self.hi, self.name = space, lo, hi, name
        self.last_w = None
        self.readers = []
        self.ovl = None


class Prog:
    ENG = ("pe", "act", "dve", "pool", "sp")

    def __init__(self, nc):
        self.nc = nc
        self.ops = {e: [] for e in self.ENG}
        self.keys = []
        self.dma_cnt = {}
        self.dma_sems = []
        self.names = {}

    def key(self, space, lo, hi, name):
        k = Key(space, lo, hi, name)
        self.keys.append(k)
        return k

    def _ovl(self, k):
        if k.ovl is None or k.ovl[0] != len(self.keys):
            lst = [o for o in self.keys if o.space == k.space and o.lo < k.hi and k.lo < o.hi]
            k.ovl = (len(self.keys), lst)
        return k.ovl[1]

    def op(self, eng, fn, reads=(), writes=(), dma=None, inc=16):
        deps = set()
        for k in reads:
            for o in self._ovl(k):
                if o.last_w is not None:
                    deps.add(o.last_w)
        for k in writes:
            for o in self._ovl(k):
                if o.last_w is not None:
                    deps.add(o.last_w)
                for r in o.readers:
                    deps.add(r)
        deps = set(("dma", d[1], self.dma_cnt[d[1]]) if d[0] == "dma" else d for d in deps)
        idx = len(self.ops[eng])
        if dma is not None:
            self.dma_cnt[dma] = self.dma_cnt.get(dma, 0) + inc
            me = ("dma", dma, self.dma_cnt[dma])
        else:
            me = ("eng", eng, idx)
        import sys as _sys
        fr = _sys._getframe(1)
        while fr.f_code.co_name in ("V", "A", "G", "T", "dma", "cmul", "cplx_axpy_step"):
            fr = fr.f_back
        rec = dict(fn=fn, deps=deps, dma=dma, me=me, marked=False, inc=inc, line=fr.f_lineno)
        self.ops[eng].append(rec)
        for k in writes:
            for o in self._ovl(k):
                if o is not k:
                    o.readers = []
            k.last_w = me
            k.readers = []
        for k in reads:
            if me[0] == "eng":
                k.readers = [r for r in k.readers if not (r[0] == "eng" and r[1] == me[1])]
            k.readers.append(me)
        return me

    def emit(self, final_waits):
        nc = self.nc
        for e in self.ENG:
            for rec in self.ops[e]:
                for d in rec["deps"]:
                    if d[0] == "eng":
                        if d[1] == "pe" and e == "pe":
                            continue
                        self.ops[d[1]][d[2]]["marked"] = True
        for d in final_waits:
            if d[0] == "eng":
                self.ops[d[1]][d[2]]["marked"] = True
        cnt_at = {}
        for e in self.ENG:
            c = 0
            for i, rec in enumerate(self.ops[e]):
                if rec["marked"] and rec["dma"] is None:
                    c += 1
                cnt_at[(e, i)] = c
        import contextlib
        with contextlib.ExitStack() as st:
            esem = {e: st.enter_context(nc.semaphore("s_" + e)) for e in self.ENG}
            dsem = {n: st.enter_context(nc.semaphore("d_" + n)) for n in self.dma_cnt}
            block = st.enter_context(nc.Block())

            def run(e, eng):
                seen = {}
                for i, rec in enumerate(self.ops[e]):
                    need = {}
                    for d in rec["deps"]:
                        if d[0] == "eng":
                            if d[1] == "pe" and e == "pe":
                                continue
                            if d[1] == e and d[2] >= i:
                                continue
                            s, v = esem[d[1]], cnt_at[(d[1], d[2])]
                        else:
                            s, v = dsem[d[1]], d[2]
                        if v > need.get(s, (0, None))[0]:
                            need[s] = (v, s)
                    for s, (v, _) in need.items():
                        if seen.get(s, 0) < v:
                            eng.wait_ge(s, v)
                            seen[s] = v
                    ins = rec["fn"](eng)
                    try:
                        self.names[ins.ins.name] = rec["line"]
                    except Exception:
                        pass
                    if rec["dma"] is not None:
                        ins.then_inc(dsem[rec["dma"]], rec["inc"])
                    elif rec["marked"]:
                        ins.then_inc(esem[e], 1)
                if e == "sp":
                    for d in final_waits:
                        if d[0] == "eng":
                            eng.wait_ge(esem[d[1]], cnt_at[(d[1], d[2])])
                        else:
                            eng.wait_ge(dsem[d[1]], d[2])

            @block.tensor
            def _(eng):
                run("pe", eng)

            @block.scalar
            def _(eng):
                run("act", eng)

            @block.vector
            def _(eng):
                run("dve", eng)

            @block.gpsimd
            def _(eng):
                run("pool", eng)

            @block.sync
            def _(eng):
                run("sp", eng)


class Buf:
    def __init__(self, P, name, shape, dtype, off, nkeys=1):
        self.P = P
        esz = 2 if dtype == BF16 else 4
        n = 1
        for s in shape:
            n *= s
        self.nbytes = n * esz
        self.off = off
        self.t = P.nc.alloc_sbuf_tensor_at(name, [128] + list(shape), dtype, offset=SBUF_BASE + off)
        self.ap = self.t.ap()
        assert SBUF_BASE + off + self.nbytes <= SBUF_LIMIT, name
        assert self.nbytes % nkeys == 0
        step = self.nbytes // nkeys
        self.k = [P.key("sb", off + i * step, off + (i + 1) * step, "%s.%d" % (name, i)) for i in range(nkeys)]

    @property
    def end(self):
        return self.off + self.nbytes


def build(nl, use_ag, debug=False):
    nc = bass.Bass("TRN2", target_bir_lowering=False)
    P = Prog(nc)

    def din(name, shape):
        return nc.dram_tensor(name, list(shape), F32, kind="ExternalInput").ap()

    xT = din("xT", [D, NTOK])
    norm_mix = din("norm_mix", [nl, D])
    w_in = din("w_in", [nl, D, D])
    w_pool = din("w_pool", [nl, 4, 128, 128])
    pool_scale = din("pool_scale", [nl, 512])
    lam_re = din("lam_re", [nl, 32, 64])
    lam_im = din("lam_im", [nl, 32, 64])
    log_dt = din("log_dt", [nl, 32])
    b_re = din("b_re", [nl, 32, 64, 16])
    b_im = din("b_im", [nl, 32, 64, 16])
    c_re = din("c_re", [nl, 32, 16, 64])
    c_im = din("c_im", [nl, 32, 16, 64])
    d_skip = din("d_skip", [nl, 512])
    w_glu = din("w_glu", [nl, 512, 512])
    b_glu = din("b_glu", [nl, 512])
    w_out = din("w_out", [nl, D, D])
    norm_ffn = din("norm_ffn", [nl, D])
    w_gate = din("w_gate", [nl, D, DFF])
    w_up = din("w_up", [nl, D, DFF])
    w_down = din("w_down", [nl, DFF, D])
    norm_final = din("norm_final", [D])
    consts = din("consts", [128, 512])
    cnt_tab = din("cnt_tab", [128, 64])
    sel = din("sel", [128, 8])
    carry_in = din("carry_in", [nl, 128, CARRY_W])
    yT = nc.dram_tensor("yT", [D, NTOK], F32, kind="ExternalOutput").ap()
    hT_out = nc.dram_tensor("hT_out", [D, NTOK], F32, kind="ExternalOutput").ap()
    carry_out = nc.dram_tensor("carry_out", [nl, 128, CARRY_W], F32, kind="ExternalOutput").ap()
    if use_ag:
        cc_src = [nc.dram_tensor("cc_src%d" % l, [128, CARRY_W], F32) for l in range(nl)]
        cc_dst = [nc.dram_tensor("cc_dst%d" % l, [8 * 128, CARRY_W], F32) for l in range(nl)]
    dk = {}

    def dkey(name):
        if name not in dk:
            dk[name] = P.key("dram", len(dk), len(dk) + 1, name)
        return dk[name]

    dbg_finals = []

    def dbg(name, buf):
        if not debug:
            return
        shp = list(buf.ap.shape)
        t = nc.dram_tensor("dbg_" + name, shp, buf.ap.dtype, kind="ExternalOutput").ap()
        dbg_finals.append(P.op("sp", lambda e: e.dma_start(out=t, in_=buf.ap), buf.k, [dkey("dbg_" + name)], dma="dbg_" + name))

    off = [0]

    def alloc(name, shape, dtype, nkeys=1, at=None):
        o = off[0] if at is None else at
        b = Buf(P, name, shape, dtype, o, nkeys)
        if at is None:
            off[0] = (b.end + 31) // 32 * 32
        return b

    h = alloc("h", [DT, NTOK], F32, nkeys=DT * TT)
    cst = alloc("cst", [512], F32)
    cntb = alloc("cntb", [4, 16], F32)
    selb = alloc("selb", [8], F32)
    g1 = alloc("g1", [nl, DT], F32)
    g2 = alloc("g2", [nl, DT], F32)
    gf = alloc("gf", [DT], F32)
    pscale = alloc("pscale", [nl, 4], F32)
    bglu = alloc("bglu", [nl, 4], F32)
    dskp = alloc("dskp", [nl, 4], F32)
    ident_bf = alloc("ident_bf", [128], BF16)
    ones_bf = alloc("ones_bf", [128], BF16)
    zeros = alloc("zeros", [2, NC1], F32)
    lamn = alloc("lamn", [3, 128], F32)
    prm = alloc("prm", [24, NPAIR], F32)
    prmi = alloc("prmi", [NPAIR], mybir.dt.int32)
    pw1 = alloc("pw1", [2, NPAIR, 16], F32)
    pw2 = alloc("pw2", [2, NPAIR, 16], F32)
    pwt = alloc("pwt", [4, NPAIR, 8], F32)
    WB = alloc("WB", [4, 2, 128], BF16)
    WC = alloc("WC", [NPAIR, 2, 32], BF16)
    diagD = alloc("diagD", [4, 128], BF16)
    Xe = alloc("Xe", [2, NPAIR, 1 + NC1], F32, nkeys=2 * NPAIR)
    Fb = alloc("Fb", [2, NPAIR, 9], F32)
    l3t = alloc("l3t", [4, NPAIR], F32)
    sin_b = alloc("sin_b", [2, NPAIR], F32)
    cblk = alloc("cblk", [CARRY_W], F32)
    gath = alloc("gath", [8, CARRY_W], F32)
    persist_end = off[0]
    w_in_b = alloc("w_in_b", [DT, D], BF16, nkeys=DT)
    w_glu_b = alloc("w_glu_b", [4, 512], BF16)
    w_pool_b = alloc("w_pool_b", [4, 128], BF16)
    common_end = off[0]
    regA = off[0]
    sq = alloc("sq", [2, 512], BF16, nkeys=2)
    rstd = alloc("rstd", [512], F32)
    xn1 = alloc("xn1", [DT, 512], BF16, nkeys=DT)
    regA_end = off[0]
    off[0] = regA
    sring = alloc("sring", [4, 2, 2, NC1], F32, nkeys=8)
    stmp = alloc("stmp", [4, 2, NC1], F32, nkeys=4)
    l2t = alloc("l2t", [4, NPAIR * 16], F32)
    off[0] = max(off[0], regA_end)
    poolreg = off[0]
    upool = alloc("upool", [4, 16 + 512], F32, nkeys=4)
    upool0 = alloc("upool0", [4, 16 + 512], F32, nkeys=4)
    ps_s = alloc("ps_s", [2, 528], F32, nkeys=2)
    dpool = alloc("dpool", [4, 512], BF16, nkeys=4)
    poolreg_end = off[0]
    off[0] = poolreg
    bnat = alloc("bnat", [2, NPAIR, 32], F32)
    bbar = alloc("bbar", [2, NPAIR, 32], F32)
    btmp = alloc("btmp", [2, NPAIR, 32], F32)
    cnat = alloc("cnat", [4, 2, 64], F32)
    cin = alloc("cin", [4, 2, 128], F32)
    assert off[0] <= poolreg_end
    off[0] = poolreg
    sbf = alloc("sbf", [2, 2, T1, NC1], BF16, nkeys=2 * 2 * 4)
    gt = alloc("gt", [1, 2, 512], F32, nkeys=2)
    assert off[0] <= poolreg_end
    off[0] = poolreg_end
    ussm = alloc("ussm", [4, T1, NC1], BF16, nkeys=16)
    ymp = alloc("ymp", [4, T1, NC1], BF16, nkeys=16)
    w_out_b = alloc("w_out_b", [DT, D], BF16, nkeys=DT, at=w_in_b.off)
    mixer_end = off[0]
    off[0] = w_glu_b.off
    sq2 = alloc("sq2", [2, 512], BF16, nkeys=2)
    rstd2 = alloc("rstd2", [512], F32)
    xn2 = alloc("xn2", [DT, 1024], BF16, nkeys=DT * 2)
    act = alloc("act", [FT, 1024], BF16, nkeys=FT * 2)
    sg = alloc("sg", [2, 512], F32, nkeys=2)
    wgu = alloc("wgu", [2, 2, DT, 128], BF16, nkeys=2)
    wd = alloc("wd", [2, FT, 128], BF16, nkeys=2)
    ffn_end = off[0]
    assert SBUF_BASE + max(mixer_end, ffn_end) <= SBUF_LIMIT, (mixer_end, ffn_end)

    ps_t = nc.alloc_psum_tensor("ps", [128, 8, 512], F32)
    ps = ps_t.ap()
    psk = [P.key("ps", b, b + 1, "ps%d" % b) for b in range(8)]

    ident = cst.ap[:, 0:128]
    m_gi = [cst.ap[:, 128:129], cst.ap[:, 129:130]]

    def V(fn, r, w):
        return P.op("dve", fn, r, w)

    def A(fn, r, w):
        return P.op("act", fn, r, w)

    def G(fn, r, w):
        return P.op("pool", fn, r, w)

    def T(fn, r, w):
        return P.op("pe", fn, r, w)

    def dma(q, out, in_, r, w, sem, **kw):
        return P.op(q, lambda e: e.dma_start(out=out, in_=in_, **kw), r, w, dma=sem)

    def hk(dt, tt):
        return h.k[dt * TT + tt]

    def bc(ap2, shape):
        return ap2.broadcast_to(shape)

    dma("sp", cst.ap, consts, [], cst.k, "cst")
    dma("sp", cntb.ap, cnt_tab.rearrange("p (g t) -> p g t", g=4), [], cntb.k, "cst")
    dma("sp", selb.ap, sel, [], selb.k, "cst")
    for dt in range(DT):
        dma("sp", h.ap[:, dt, :], xT[dt * 128:(dt + 1) * 128, :], [], [hk(dt, t) for t in range(TT)], "hload")

    def small_T(dst, src, n):
        dma("act", dst, src.rearrange("l (t c) -> c l t", c=128), [], [], "cst",
            allow_slow_non_contiguous=True)

    small_T(g1.ap, norm_mix, DT)
    small_T(g2.ap, norm_ffn, DT)
    small_T(pscale.ap, pool_scale, 4)
    small_T(bglu.ap, b_glu, 4)
    small_T(dskp.ap, d_skip, 4)
    P.op("act", lambda e: e.dma_start(out=gf.ap, in_=norm_final.rearrange("(t c) -> c t", c=128),
                                      allow_slow_non_contiguous=True),
         [], g1.k + g2.k + gf.k + pscale.k + bglu.k + dskp.k, dma="cst")
    V(lambda e: e.tensor_copy(out=ident_bf.ap, in_=ident), cst.k, ident_bf.k)
    V(lambda e: e.memset(ones_bf.ap, 1.0), [], ones_bf.k)
    V(lambda e: e.memset(zeros.ap, 0.0), [], zeros.k)

    def norm_tile(tt, gain_ap, sqb, rstdb, xnb, xn_cols, xn_keys, bank):
        for dt in range(DT):
            s = dt % 2
            A(lambda e, dt=dt, s=s: e.activation(out=sqb.ap[:, s, :], in_=h.ap[:, dt, tt * 512:(tt + 1) * 512],
                                                 func=AF.Square),
              [hk(dt, tt)], [sqb.k[s]])
            T(lambda e, dt=dt, s=s: e.matmul(ps[:, bank, :], lhsT=ones_bf.ap, rhs=sqb.ap[:, s, :],
                                             start=(dt == 0), stop=(dt == DT - 1)),
              [sqb.k[s], ones_bf.k[0]], [psk[bank]])
        V(lambda e: e.tensor_scalar(out=rstdb.ap, in0=ps[:, bank, :], scalar1=1.0 / D, scalar2=EPS,
                                    op0=ALU.mult, op1=ALU.add),
          [psk[bank]], rstdb.k)
        A(lambda e: e.activation(out=rstdb.ap, in_=rstdb.ap, func=AF.Sqrt), rstdb.k, rstdb.k)
        V(lambda e: e.reciprocal(out=rstdb.ap, in_=rstdb.ap), rstdb.k, rstdb.k)
        for dt in range(DT):
            V(lambda e, dt=dt: e.scalar_tensor_tensor(out=xnb.ap[:, dt, xn_cols], in0=h.ap[:, dt, tt * 512:(tt + 1) * 512],
                                                      scalar=gain_ap[:, dt:dt + 1], in1=rstdb.ap,
                                                      op0=ALU.mult, op1=ALU.mult),
              [hk(dt, tt), rstdb.k[0], g1.k[0]], [xn_keys[dt]])

    POOLW = (2, 4, 8, 16)

    def pool_tile(l, tt, ub, bank0):
        for g in range(4):
            w = POOLW[g]
            u = ub.ap[:, g, :]
            cur, curk = u, ub.k[g]
            sh = 1
            si = 0
            while sh < w:
                dst = ps_s.ap[:, si, :]
                V(lambda e, cur=cur, dst=dst, sh=sh: e.tensor_add(out=dst[:, sh:528], in0=cur[:, sh:528], in1=cur[:, 0:528 - sh]),
                  [curk], [ps_s.k[si]])
                cur, curk = dst, ps_s.k[si]
                si ^= 1
                sh *= 2
            V(lambda e, cur=cur, u=u, g=g, w=w: e.scalar_tensor_tensor(out=dpool.ap[:, g, :], in0=cur[:, 16:528], scalar=1.0 / w,
                                                                        in1=u[:, 16:528], op0=ALU.mult, op1=ALU.subtract),
              [curk, ub.k[g]], [dpool.k[g]])
            if tt == 0:
                dst = ps_s.ap[:, si, :]
                V(lambda e, cur=cur, dst=dst, g=g: e.tensor_mul(out=dst[:, 0:16], in0=cur[:, 16:32], in1=cntb.ap[:, g, :]),
                  [curk, cntb.k[0]], [ps_s.k[si]])
                V(lambda e, dst=dst, u=u, g=g: e.tensor_sub(out=dpool.ap[:, g, 0:16], in0=dst[:, 0:16], in1=u[:, 16:32]),
                  [ps_s.k[si], ub.k[g]], [dpool.k[g]])
            bank = bank0 + (g % 2)
            T(lambda e, g=g, bank=bank: e.matmul(ps[:, bank, :], lhsT=w_pool_b.ap[:, g, :], rhs=dpool.ap[:, g, :],
                                                 start=True, stop=True),
              [dpool.k[g], w_pool_b.k[0]], [psk[bank]])
            A(lambda e, g=g, bank=bank: e.activation(out=ymp.ap[:, g, :, tt * 32:(tt + 1) * 32],
                                                     in_=ps[:, bank, :].rearrange("p (c j) -> p j c", j=T1),
                                                     func=AF.Copy, scale=pscale.ap[:, l, g:g + 1]),
              [psk[bank], pscale.k[0]], [ymp.k[g * 4 + b] for b in range(4)])

    def cmul(eng_op, out_r, out_i, a_r, a_i, b_r, b_i, t1, t2, rk, wk, tk):
        eng_op(lambda e: e.tensor_mul(out=t1, in0=a_r, in1=b_r), rk, tk)
        eng_op(lambda e: e.tensor_mul(out=t2, in0=a_i, in1=b_i), rk, tk)
        eng_op(lambda e: e.tensor_sub(out=out_r, in0=t1, in1=t2), tk, wk)
        eng_op(lambda e: e.tensor_mul(out=t1, in0=a_r, in1=b_i), rk, tk)
        eng_op(lambda e: e.tensor_mul(out=t2, in0=a_i, in1=b_r), rk, tk)
        eng_op(lambda e: e.tensor_add(out=out_i, in0=t1, in1=t2), tk, wk)

    def ssm_setup(l):
        S = lambda i: prm.ap[:, i, :]
        pk = prm.k
        for i, src in enumerate((lam_re, lam_im)):
            dma("sp", lamn.ap[0:16, i, :], src[l].rearrange("(pr gi) p -> pr (gi p)", gi=2), [], lamn.k, "lam")
        dma("sp", lamn.ap[0:1, 2, 0:32], log_dt[l].unsqueeze(0), [], lamn.k, "lam")
        T(lambda e: e.matmul(ps[:, 0, 32:64], lhsT=cst.ap[0:1, 130:258], rhs=lamn.ap[0:1, 2, 0:32], start=True, stop=True),
          lamn.k + cst.k, [psk[0]])
        for i in range(2):
            T(lambda e, i=i: e.transpose(ps[:, 0, i * 16:(i + 1) * 16], lamn.ap[0:16, i, :], ident[0:16, 0:16]),
              lamn.k + cst.k, [psk[0]])
        V(lambda e: e.tensor_copy(out=prm.ap[:, 0:2, :], in_=ps[:, 0, 0:32].rearrange("p (a b) -> p a b", a=2)),
          [psk[0]], pk)
        for gi in range(2):
            V(lambda e, gi=gi: e.tensor_copy(out=prm.ap[gi * 64:(gi + 1) * 64, 2, :],
                                             in_=ps[gi * 64:(gi + 1) * 64, 0, 32:64].rearrange("p (pr gi) -> p pr gi", gi=2)[:, :, gi]),
              [psk[0]], pk)
        lr, li, dtt = S(0), S(1), S(2)
        A(lambda e: e.activation(out=dtt, in_=dtt, func=AF.Exp), pk, pk)
        V(lambda e: e.tensor_mul(out=S(3), in0=lr, in1=dtt), pk, pk)
        A(lambda e: e.activation(out=S(3), in_=S(3), func=AF.Exp), pk, pk)
        V(lambda e: e.tensor_mul(out=S(4), in0=li, in1=dtt), pk, pk)
        V(lambda e: e.tensor_scalar(out=S(5), in0=S(4), scalar1=0.125, scalar2=3.1415, op0=ALU.mult, op1=ALU.min), pk, pk)
        V(lambda e: e.tensor_scalar(out=S(6), in0=S(5), scalar1=-1.0, scalar2=0.5 * math.pi, op0=ALU.mult, op1=ALU.add), pk, pk)
        A(lambda e: e.activation(out=S(5), in_=S(5), func=AF.Sin), pk, pk)
        A(lambda e: e.activation(out=S(6), in_=S(6), func=AF.Sin), pk, pk)
        for _ in range(3):
            V(lambda e: e.tensor_mul(out=S(10), in0=S(5), in1=S(6)), pk, pk)
            V(lambda e: e.tensor_mul(out=S(11), in0=S(5), in1=S(5)), pk, pk)
            V(lambda e: e.tensor_mul(out=S(6), in0=S(6), in1=S(6)), pk, pk)
            V(lambda e: e.tensor_sub(out=S(6), in0=S(6), in1=S(11)), pk, pk)
            V(lambda e: e.tensor_scalar(out=S(5), in0=S(10), scalar1=2.0, scalar2=None, op0=ALU.mult), pk, pk)
        ar, ai, nai = S(7), S(8), S(9)
        V(lambda e: e.tensor_mul(out=ar, in0=S(6), in1=S(3)), pk, pk)
        V(lambda e: e.tensor_mul(out=ai, in0=S(5), in1=S(3)), pk, pk)
        V(lambda e: e.scalar_tensor_tensor(out=nai, in0=S(5), scalar=-1.0, in1=S(3), op0=ALU.mult, op1=ALU.mult), pk, pk)
        V(lambda e: e.tensor_mul(out=S(10), in0=lr, in1=lr), pk, pk)
        V(lambda e: e.tensor_mul(out=S(11), in0=li, in1=li), pk, pk)
        V(lambda e: e.tensor_add(out=S(10), in0=S(10), in1=S(11)), pk, pk)
        V(lambda e: e.reciprocal(out=S(10), in_=S(10)), pk, pk)
        V(lambda e: e.tensor_scalar_add(out=S(11), in0=ar, scalar1=-1.0), pk, pk)
        V(lambda e: e.tensor_mul(out=S(12), in0=S(11), in1=lr), pk, pk)
        V(lambda e: e.tensor_mul(out=S(13), in0=ai, in1=li), pk, pk)
        V(lambda e: e.tensor_add(out=S(12), in0=S(12), in1=S(13)), pk, pk)
        V(lambda e: e.tensor_mul(out=S(12), in0=S(12), in1=S(10)), pk, pk)
        V(lambda e: e.tensor_mul(out=S(13), in0=ai, in1=lr), pk, pk)
        V(lambda e: e.tensor_mul(out=S(14), in0=S(11), in1=li), pk, pk)
        V(lambda e: e.tensor_sub(out=S(13), in0=S(13), in1=S(14)), pk, pk)
        V(lambda e: e.tensor_mul(out=S(13), in0=S(13), in1=S(10)), pk, pk)
        V(lambda e: e.memset(bnat.ap, 0.0), [], bnat.k)
        for part, src in enumerate((b_re, b_im)):
            for gi in range(2):
                dma("sp", bnat.ap[gi * 64:(gi + 1) * 64, part, :, gi * 16:(gi + 1) * 16],
                    src[l].rearrange("(pr gi) p h -> gi p pr h", gi=2)[gi],
                    [], bnat.k, "bload", allow_slow_non_contiguous=True)
        sh3 = [128, NPAIR, 32]
        cr = S(12).unsqueeze(2).broadcast_to(sh3)
        ci = S(13).unsqueeze(2).broadcast_to(sh3)
        cmul(V, bbar.ap[:, 0], bbar.ap[:, 1], cr, ci, bnat.ap[:, 0], bnat.ap[:, 1], btmp.ap[:, 0], btmp.ap[:, 1],
             pk + bnat.k, bbar.k, btmp.k)
        for t4 in range(4):
            for part in range(2):
                bank = 1 + (t4 * 2 + part) % 2
                T(lambda e, t4=t4, part=part, bank=bank: e.transpose(
                    ps[:, bank, 0:128], bbar.ap[:, part, t4 * 4:(t4 + 1) * 4, :].rearrange("p a b -> p (a b)"), ident),
                  bbar.k + cst.k, [psk[bank]])
                A(lambda e, t4=t4, part=part, bank=bank: e.copy(out=WB.ap[:, t4, part, :], in_=ps[:, bank, 0:128]),
                  [psk[bank]], WB.k)
        for part, src in enumerate((c_re, c_im)):
            dma("sp", cnat.ap[:, :, part, :], src[l].rearrange("(t r) h p -> (r h) t p", t=4), [], cnat.k, "cload")
        for gi in range(2):
            V(lambda e, gi=gi: e.tensor_scalar(out=cin.ap[:, :, :, gi * 64:(gi + 1) * 64], in0=cnat.ap,
                                               scalar1=m_gi[gi], scalar2=None, op0=ALU.mult),
              cnat.k + cst.k, cin.k)
        for t4 in range(4):
            for part in range(2):
                bank = 1 + (t4 * 2 + part) % 2
                T(lambda e, t4=t4, part=part, bank=bank: e.transpose(ps[:, bank, 0:128], cin.ap[:, t4, part, :], ident),
                  cin.k + cst.k, [psk[bank]])
                sc = 1.0 if part == 0 else -1.0
                A(lambda e, t4=t4, part=part, bank=bank, sc=sc: e.activation(
                    out=WC.ap[:, t4 * 4:(t4 + 1) * 4, part, :], in_=ps[:, bank, 0:128].rearrange("p (a b) -> p a b", a=4),
                    func=AF.Copy, scale=sc),
                  [psk[bank]], WC.k)
        for t4 in range(4):
            V(lambda e, t4=t4: e.tensor_scalar(out=diagD.ap[:, t4, :], in0=ident, scalar1=dskp.ap[:, l, t4:t4 + 1],
                                               scalar2=None, op0=ALU.mult),
              cst.k + dskp.k, diagD.k)
        def powers(pw, base_r, base_i, rk):
            V(lambda e: e.tensor_copy(out=pw.ap[:, 0, :, 0], in_=base_r), rk, pw.k)
            V(lambda e: e.tensor_copy(out=pw.ap[:, 1, :, 0], in_=base_i), rk, pw.k)
            n = 1
            while n < 16:
                shp = [128, NPAIR, n]
                br = pw.ap[:, 0, :, n - 1:n].broadcast_to(shp)
                bi = pw.ap[:, 1, :, n - 1:n].broadcast_to(shp)
                cmul(V, pw.ap[:, 0, :, n:2 * n], pw.ap[:, 1, :, n:2 * n], pw.ap[:, 0, :, 0:n], pw.ap[:, 1, :, 0:n], br, bi,
                     pwt.ap[:, 0, :, 0:n], pwt.ap[:, 1, :, 0:n], pw.k, pw.k, pwt.k)
                n *= 2
        powers(pw1, ar, ai, pk)
        powers(pw2, pw1.ap[:, 0, :, 15], pw1.ap[:, 1, :, 15], pw1.k)

    def ssm_group(prs, final):
        for blk in range(4):
            for L, pr in enumerate(prs):
                t4, q = pr // 4, pr % 4
                zb = 2 * L
                for part in range(2):
                    T(lambda e, part=part, blk=blk, zb=zb, t4=t4, q=q: e.matmul(
                        ps[:, zb + part, :], lhsT=WB.ap[32 * q:32 * q + 32, t4, part, :],
                        rhs=ussm.ap[32 * q:32 * q + 32, t4, 4 * blk:4 * blk + 4, :].rearrange("p a b -> p (a b)"),
                        start=True, stop=True, tile_position=(32 * q, 0)),
                      [ussm.k[t4 * 4 + blk], WB.k[0]], [psk[zb + part]])
            for jj in range(4):
                j = 4 * blk + jj
                ctx = []
                for L, pr in enumerate(prs):
                    zb = 2 * L
                    c = dict(pr=pr, zb=zb, bu=ps[:, zb:zb + 2, jj * 128:(jj + 1) * 128],
                             ar=prm.ap[:, 7, pr:pr + 1], ai=prm.ap[:, 8, pr:pr + 1], nai=prm.ap[:, 9, pr:pr + 1])
                    if j == 0:
                        if final:
                            c["prev"], c["prevk"] = Xe.ap[:, :, pr, 0:NC1], [Xe.k[pr], Xe.k[NPAIR + pr]]
                        else:
                            c["prev"], c["prevk"] = zeros.ap, zeros.k
                    else:
                        c["prev"], c["prevk"] = sring.ap[:, L, (j - 1) % 2], [sring.k[L * 2 + (j - 1) % 2]]
                    if (not final) and j == T1 - 1:
                        c["cur"], c["curk"] = Xe.ap[:, :, pr, 1:1 + NC1], [Xe.k[pr], Xe.k[NPAIR + pr]]
                    else:
                        c["cur"], c["curk"] = sring.ap[:, L, j % 2], [sring.k[L * 2 + j % 2]]
                    c["tm"], c["tmk"] = stmp.ap[:, L], [stmp.k[L]]
                    ctx.append(c)
                for c in ctx:
                    V(lambda e, c=c: e.scalar_tensor_tensor(out=c["tm"], in0=c["prev"], scalar=c["ar"], in1=c["bu"],
                                                            op0=ALU.mult, op1=ALU.add),
                      c["prevk"] + [psk[c["zb"]], psk[c["zb"] + 1], prm.k[0]], c["tmk"])
                for c in ctx:
                    V(lambda e, c=c: e.scalar_tensor_tensor(out=c["cur"][:, 0], in0=c["prev"][:, 1], scalar=c["nai"], in1=c["tm"][:, 0],
                                                            op0=ALU.mult, op1=ALU.add),
                      c["prevk"] + c["tmk"], c["curk"])
                for c in ctx:
                    V(lambda e, c=c: e.scalar_tensor_tensor(out=c["cur"][:, 1], in0=c["prev"][:, 0], scalar=c["ai"], in1=c["tm"][:, 1],
                                                            op0=ALU.mult, op1=ALU.add),
                      c["prevk"] + c["tmk"], c["curk"])
                if final:
                    for L, c in enumerate(ctx):
                        A(lambda e, c=c, j=j, L=L: e.copy(out=sbf.ap[:, L, :, j, :], in_=c["cur"]),
                          c["curk"], [sbf.k[(L * 2 + part) * 4 + blk] for part in range(2)])
            if final:
                yb = 4 + blk
                for L, pr in enumerate(prs):
                    t4, q = pr // 4, pr % 4
                    rhs_u = ussm.ap[32 * q:32 * q + 32, t4, 4 * blk:4 * blk + 4, :].rearrange("p a b -> p (a b)")
                    for part in range(2):
                        T(lambda e, part=part, blk=blk, yb=yb, pr=pr, q=q, L=L: e.matmul(
                            ps[32 * q:32 * q + 32, yb, :], lhsT=WC.ap[:, pr, part, :],
                            rhs=sbf.ap[:, L, part, 4 * blk:4 * blk + 4, :].rearrange("p a b -> p (a b)"),
                            start=(part == 0), stop=False, tile_position=(0, 32 * q)),
                          [sbf.k[(L * 2 + part) * 4 + blk], WC.k[0]], [psk[yb]])
                    T(lambda e, yb=yb, rhs_u=rhs_u, t4=t4, q=q: e.matmul(
                        ps[32 * q:32 * q + 32, yb, :], lhsT=diagD.ap[32 * q:32 * q + 32, t4, 32 * q:32 * q + 32], rhs=rhs_u,
                        start=False, stop=True, tile_position=(32 * q, 32 * q)),
                      [ussm.k[t4 * 4 + blk], diagD.k[0]], [psk[yb]])

    def gelu_tile(t4):
        for blk in range(4):
            yb = 4 + blk
            y = ps[:, yb, :]
            t_a = gt.ap[:, 0, 0, :]
            t_b = gt.ap[:, 0, 1, :]
            tk = [gt.k[0], gt.k[1]]
            A(lambda e, y=y, t_a=t_a: e.activation(out=t_a, in_=y, func=AF.Square), [psk[yb]], tk)
            V(lambda e, t_a=t_a: e.tensor_scalar(out=t_a, in0=t_a, scalar1=0.044715, scalar2=1.0, op0=ALU.mult, op1=ALU.add), tk, tk)
            V(lambda e, y=y, t_a=t_a: e.tensor_mul(out=t_a, in0=t_a, in1=y), tk + [psk[yb]], tk)
            A(lambda e, t_a=t_a, t_b=t_b: e.activation(out=t_b, in_=t_a, func=AF.Sigmoid, scale=2.0 * math.sqrt(2.0 / math.pi)), tk, tk)
            V(lambda e, y=y, t_b=t_b, blk=blk: e.tensor_mul(out=ussm.ap[:, t4, 4 * blk:4 * blk + 4, :].rearrange("p a b -> p (a b)"),
                                                            in0=t_b, in1=y),
              tk + [psk[yb]], [ussm.k[t4 * 4 + blk]])

    def cplx_axpy_step(cur_r, cur_i, prev_r, prev_i, m_r, m_i, shape, rk, wk):
        t = [l2t.ap[:, i, 0:shape[0] * shape[1]].rearrange("p (a b) -> p a b", a=shape[0]) for i in range(4)]
        tk = l2t.k
        V(lambda e: e.tensor_mul(out=t[0], in0=m_r, in1=prev_r), rk, tk)
        V(lambda e: e.tensor_mul(out=t[1], in0=m_i, in1=prev_i), rk, tk)
        V(lambda e: e.tensor_sub(out=t[0], in0=t[0], in1=t[1]), tk, tk)
        V(lambda e: e.tensor_add(out=cur_r, in0=cur_r, in1=t[0]), tk + wk, wk)
        V(lambda e: e.tensor_mul(out=t[2], in0=m_i, in1=prev_r), rk, tk)
        V(lambda e: e.tensor_mul(out=t[3], in0=m_r, in1=prev_i), rk, tk)
        V(lambda e: e.tensor_add(out=t[2], in0=t[2], in1=t[3]), tk, tk)
        V(lambda e: e.tensor_add(out=cur_i, in0=cur_i, in1=t[2]), tk + wk, wk)

    def level2():
        al_r = pw1.ap[:, 0, :, 15:16].broadcast_to([128, NPAIR, 8])
        al_i = pw1.ap[:, 1, :, 15:16].broadcast_to([128, NPAIR, 8])
        for j2 in range(1, 16):
            cr = Xe.ap[:, 0, :, 1 + j2:1 + NC1:16]
            ci = Xe.ap[:, 1, :, 1 + j2:1 + NC1:16]
            pr_ = Xe.ap[:, 0, :, j2:NC1:16]
            pi_ = Xe.ap[:, 1, :, j2:NC1:16]
            cplx_axpy_step(cr, ci, pr_, pi_, al_r, al_i, [NPAIR, 8], Xe.k + pw1.k, Xe.k)

    def level3(init_r, init_i, initk):
        V(lambda e: e.tensor_copy(out=Fb.ap[:, 0, :, 0], in_=init_r), initk, Fb.k)
        V(lambda e: e.tensor_copy(out=Fb.ap[:, 1, :, 0], in_=init_i), initk, Fb.k)
        a2r = pw2.ap[:, 0, :, 15]
        a2i = pw2.ap[:, 1, :, 15]
        t = [l3t.ap[:, i] for i in range(4)]
        tk = l3t.k
        for c2 in range(8):
            er = Xe.ap[:, 0, :, 16 * c2 + 16]
            ei = Xe.ap[:, 1, :, 16 * c2 + 16]
            fr, fi = Fb.ap[:, 0, :, c2], Fb.ap[:, 1, :, c2]
            nr, ni = Fb.ap[:, 0, :, c2 + 1], Fb.ap[:, 1, :, c2 + 1]
            rk = Fb.k + pw2.k + Xe.k
            V(lambda e, fr=fr: e.tensor_mul(out=t[0], in0=a2r, in1=fr), rk, tk)
            V(lambda e, fi=fi: e.tensor_mul(out=t[1], in0=a2i, in1=fi), rk, tk)
            V(lambda e: e.tensor_sub(out=t[0], in0=t[0], in1=t[1]), tk, tk)
            V(lambda e, nr=nr, er=er: e.tensor_add(out=nr, in0=t[0], in1=er), tk + rk, Fb.k)
            V(lambda e, fr=fr: e.tensor_mul(out=t[2], in0=a2i, in1=fr), rk, tk)
            V(lambda e, fi=fi: e.tensor_mul(out=t[3], in0=a2r, in1=fi), rk, tk)
            V(lambda e: e.tensor_add(out=t[2], in0=t[2], in1=t[3]), tk, tk)
            V(lambda e, ni=ni, ei=ei: e.tensor_add(out=ni, in0=t[2], in1=ei), tk + rk, Fb.k)

    def propagate():
        for c2 in range(8):
            cr = Xe.ap[:, 0, :, 1 + 16 * c2:17 + 16 * c2]
            ci = Xe.ap[:, 1, :, 1 + 16 * c2:17 + 16 * c2]
            fr = Fb.ap[:, 0, :, c2:c2 + 1].broadcast_to([128, NPAIR, 16])
            fi = Fb.ap[:, 1, :, c2:c2 + 1].broadcast_to([128, NPAIR, 16])
            cplx_axpy_step(cr, ci, fr, fi, pw2.ap[:, 0], pw2.ap[:, 1], [NPAIR, 16], Xe.k + pw2.k + Fb.k, Xe.k)
        V(lambda e: e.tensor_copy(out=Xe.ap[:, :, :, 0], in_=Fb.ap[:, :, :, 0]), Fb.k, Xe.k)

    def layer(l, last):
        for kt in range(DT):
            dma("pool", w_in_b.ap[:, kt, :], w_in[l, kt * 128:(kt + 1) * 128, :], [], [w_in_b.k[kt]], "w_in")
        dma("pool", w_pool_b.ap, w_pool[l].rearrange("g c d -> c g d"), [], w_pool_b.k, "w_pool")
        dma("pool", w_glu_b.ap, w_glu[l].rearrange("(kt p) n -> p kt n", p=128), [], w_glu_b.k, "w_glu")
        ssm_setup(l)
        if l == 0:
            for nm, b in (("prm", prm), ("pw1", pw1), ("pw2", pw2), ("WB", WB), ("WC", WC), ("diagD", diagD)):
                dbg(nm, b)
        for tt in range(TT):
            norm_tile(tt, g1.ap[:, l, :], sq, rstd, xn1, slice(0, 512), xn1.k, 7)
            ub = upool0 if tt == 0 else upool
            for n in range(8):
                bank = n % 4
                for kt in range(DT):
                    T(lambda e, n=n, kt=kt, bank=bank: e.matmul(ps[:, bank, :], lhsT=w_in_b.ap[:, kt, n * 128:(n + 1) * 128],
                                                                rhs=xn1.ap[:, kt, :], start=(kt == 0), stop=(kt == DT - 1)),
                      [w_in_b.k[kt], xn1.k[kt]], [psk[bank]])
                if n < 4:
                    A(lambda e, n=n, bank=bank, ub=ub: e.copy(out=ub.ap[:, n, 16:528], in_=ps[:, bank, :]), [psk[bank]], [ub.k[n]])
                else:
                    A(lambda e, n=n, bank=bank, tt=tt: e.copy(out=ussm.ap[:, n - 4, :, tt * 32:(tt + 1) * 32],
                                                       in_=ps[:, bank, :].rearrange("p (c j) -> p j c", j=T1)),
                      [psk[bank]], [ussm.k[(n - 4) * 4 + b] for b in range(4)])
            if tt == 1:
                V(lambda e: e.tensor_copy(out=upool.ap[:, :, 0:16], in_=upool0.ap[:, :, 512:528]), upool0.k, upool.k)
            if tt >= 1:
                pool_tile(l, tt, upool, 4)
                if tt < TT - 1:
                    V(lambda e: e.tensor_copy(out=ps_s.ap[:, 0, 0:64].rearrange("p (g t) -> p g t", g=4), in_=upool.ap[:, :, 512:528]),
                      upool.k, [ps_s.k[0]])
                    V(lambda e: e.tensor_copy(out=upool.ap[:, :, 0:16], in_=ps_s.ap[:, 0, 0:64].rearrange("p (g t) -> p g t", g=4)),
                      [ps_s.k[0]], upool.k)
        for kt in range(DT):
            dma("pool", w_out_b.ap[:, kt, :], w_out[l, kt * 128:(kt + 1) * 128, :], [], [w_out_b.k[kt]], "w_out")
        if l == 0:
            dbg("ussm", ussm)
        for t4 in range(4):
            ssm_group([t4 * 4 + q for q in range(4)], False)
        if l == 0:
            dbg("Xe1", Xe)
        level2()
        if l == 0:
            dbg("Xe1b", Xe)
        level3(zeros.ap[:, 0, 0:NPAIR], zeros.ap[:, 0, 0:NPAIR], zeros.k)
        V(lambda e: e.tensor_copy(out=cblk.ap[:, 0:32].rearrange("p (a b) -> p a b", a=2), in_=Fb.ap[:, :, :, 8]), Fb.k, cblk.k)
        V(lambda e: e.tensor_copy(out=cblk.ap[:, 32:96].rearrange("p (g t) -> p g t", g=4), in_=upool.ap[:, :, 512:528]),
          upool.k, cblk.k)
        dma("sp", carry_out[l], cblk.ap, cblk.k, [dkey("carry_out")], "carry_o")
        if use_ag:
            dma("pool", cc_src[l].ap(), cblk.ap, cblk.k, [dkey("cc_src%d" % l)], "cc_s%d" % l)
            P.op("pool", lambda e: e.collective_compute("AllGather", ALU.bypass, replica_groups=[list(range(8))],
                                                        ins=[cc_src[l].ap().opt()], outs=[cc_dst[l].ap().opt()]),
                 [dkey("cc_src%d" % l)], [dkey("cc_dst%d" % l)], dma="ccs%d" % l, inc=1)
            dma("pool", gath.ap, cc_dst[l].ap().rearrange("(r p) w -> p r w", p=128), [dkey("cc_dst%d" % l)], gath.k, "cc_g%d" % l)
            V(lambda e: e.tensor_scalar(out=cblk.ap, in0=gath.ap[:, 0, :], scalar1=selb.ap[:, 0:1], scalar2=None, op0=ALU.mult),
              gath.k + selb.k, cblk.k)
            for r in range(1, 8):
                V(lambda e, r=r: e.scalar_tensor_tensor(out=cblk.ap, in0=gath.ap[:, r, :], scalar=selb.ap[:, r:r + 1], in1=cblk.ap,
                                                        op0=ALU.mult, op1=ALU.add),
                  gath.k + selb.k + cblk.k, cblk.k)
        else:
            dma("sp", cblk.ap, carry_in[l], cblk.k, cblk.k, "carry_i")
        V(lambda e: e.tensor_copy(out=sin_b.ap, in_=cblk.ap[:, 0:32].rearrange("p (a b) -> p a b", a=2)), cblk.k, sin_b.k)
        V(lambda e: e.tensor_copy(out=upool0.ap[:, :, 0:16], in_=cblk.ap[:, 32:96].rearrange("p (g t) -> p g t", g=4)),
          cblk.k, upool0.k)
        level3(sin_b.ap[:, 0], sin_b.ap[:, 1], sin_b.k)
        propagate()
        if l == 0:
            dbg("Xe2", Xe)
            dbg("Fb", Fb)
        pool_tile(l, 0, upool0, 2)
        for t4 in range(4):
            ssm_group([t4 * 4, t4 * 4 + 1], True)
            ssm_group([t4 * 4 + 2, t4 * 4 + 3], True)
            gelu_tile(t4)
        for blk in range(4):
            for n in range(4):
                bank = n
                for kt in range(4):
                    T(lambda e, n=n, kt=kt, blk=blk, bank=bank: e.matmul(
                        ps[:, bank, :], lhsT=w_glu_b.ap[:, kt, n * 128:(n + 1) * 128],
                        rhs=ussm.ap[:, kt, 4 * blk:4 * blk + 4, :].rearrange("p a b -> p (a b)"),
                        start=(kt == 0), stop=(kt == 3)),
                      [w_glu_b.k[0], ussm.k[kt * 4 + blk]], [psk[bank]])
            for n in range(4):
                s = n % 2
                A(lambda e, n=n, s=s: e.activation(out=gt.ap[:, 0, s, :], in_=ps[:, n, :], func=AF.Sigmoid, bias=bglu.ap[:, l, n:n + 1]),
                  [psk[n], bglu.k[0]], [gt.k[s]])
                yv = ussm.ap[:, n, 4 * blk:4 * blk + 4, :].rearrange("p a b -> p (a b)")
                V(lambda e, yv=yv, s=s: e.tensor_mul(out=yv, in0=yv, in1=gt.ap[:, 0, s, :]),
                  [gt.k[s], ussm.k[n * 4 + blk]], [ussm.k[n * 4 + blk]])
        for blk in range(4):
            for n in range(DT):
                bank = (blk * DT + n) % 8
                for kt in range(DT):
                    src = ymp if kt < 4 else ussm
                    T(lambda e, n=n, kt=kt, blk=blk, bank=bank, src=src: e.matmul(
                        ps[:, bank, :], lhsT=w_out_b.ap[:, kt, n * 128:(n + 1) * 128],
                        rhs=src.ap[:, kt % 4, 4 * blk:4 * blk + 4, :].rearrange("p a b -> p (a b)"),
                        start=(kt == 0), stop=(kt == DT - 1)),
                      [w_out_b.k[kt], src.k[(kt % 4) * 4 + blk]], [psk[bank]])
                hv = h.ap[:, n, :].rearrange("p (c j) -> p j c", j=T1)[:, 4 * blk:4 * blk + 4, :]
                V(lambda e, hv=hv, bank=bank: e.tensor_add(out=hv, in0=hv, in1=ps[:, bank, :].rearrange("p (a b) -> p a b", a=4)),
                  [psk[bank]] + [hk(n, t) for t in range(TT)], [hk(n, t) for t in range(TT)])
        for sub in range(2):
            for tl in range(2):
                tt = sub * 2 + tl
                norm_tile(tt, g2.ap[:, l, :], sq2, rstd2, xn2, slice(tl * 512, (tl + 1) * 512),
                          [xn2.k[dt * 2 + tl] for dt in range(DT)], 7)
            for f in range(FT):
                slot = f % 2
                for gi, wsrc in enumerate((w_gate, w_up)):
                    dma("pool", wgu.ap[:, slot, gi], wsrc[l, :, f * 128:(f + 1) * 128].rearrange("(kt p) n -> p kt n", p=128),
                        [], [wgu.k[slot]], "wgu%d" % slot)
                for tl in range(2):
                    bg = (f % 2) * 4 + tl * 2
                    for gi in range(2):
                        for kt in range(DT):
                            T(lambda e, gi=gi, kt=kt, tl=tl, slot=slot, bg=bg: e.matmul(
                                ps[:, bg + gi, :], lhsT=wgu.ap[:, slot, gi, kt, :], rhs=xn2.ap[:, kt, tl * 512:(tl + 1) * 512],
                                start=(kt == 0), stop=(kt == DT - 1)),
                              [wgu.k[slot], xn2.k[kt * 2 + tl]], [psk[bg + gi]])
                    s = tl
                    A(lambda e, bg=bg, s=s: e.activation(out=sg.ap[:, s, :], in_=ps[:, bg, :], func=AF.Silu), [psk[bg]], [sg.k[s]])
                    V(lambda e, bg=bg, s=s, f=f, tl=tl: e.tensor_mul(out=act.ap[:, f, tl * 512:(tl + 1) * 512], in0=sg.ap[:, s, :],
                                                                     in1=ps[:, bg + 1, :]),
                      [sg.k[s], psk[bg + 1]], [act.k[f * 2 + tl]])
            for n in range(DT):
                slot = n % 2
                dma("pool", wd.ap[:, slot], w_down[l, :, n * 128:(n + 1) * 128].rearrange("(kt p) n -> p kt n", p=128),
                    [], [wd.k[slot]], "wd%d" % slot)
                for tl in range(2):
                    tt = sub * 2 + tl
                    bank = (n * 2 + tl) % 8
                    for f in range(FT):
                        T(lambda e, f=f, tl=tl, slot=slot, bank=bank: e.matmul(
                            ps[:, bank, :], lhsT=wd.ap[:, slot, f, :], rhs=act.ap[:, f, tl * 512:(tl + 1) * 512],
                            start=(f == 0), stop=(f == FT - 1)),
                          [wd.k[slot], act.k[f * 2 + tl]], [psk[bank]])
                    hv = h.ap[:, n, tt * 512:(tt + 1) * 512]
                    V(lambda e, hv=hv, bank=bank: e.tensor_add(out=hv, in0=hv, in1=ps[:, bank, :]),
                      [psk[bank], hk(n, tt)], [hk(n, tt)])

    for l in range(nl):
        layer(l, l == nl - 1)

    finals = []
    for dt in range(DT):
        finals.append(dma("sp", hT_out[dt * 128:(dt + 1) * 128, :], h.ap[:, dt, :], [hk(dt, t) for t in range(TT)],
                          [dkey("hT_out")], "out_h"))
    for tt in range(TT):
        for dt in range(DT):
            s = dt % 2
            A(lambda e, dt=dt, s=s, tt=tt: e.activation(out=sq2.ap[:, s, :], in_=h.ap[:, dt, tt * 512:(tt + 1) * 512], func=AF.Square),
              [hk(dt, tt)], [sq2.k[s]])
            T(lambda e, dt=dt, s=s: e.matmul(ps[:, 7, :], lhsT=ones_bf.ap, rhs=sq2.ap[:, s, :], start=(dt == 0), stop=(dt == DT - 1)),
              [sq2.k[s], ones_bf.k[0]], [psk[7]])
        V(lambda e: e.tensor_scalar(out=rstd2.ap, in0=ps[:, 7, :], scalar1=1.0 / D, scalar2=EPS, op0=ALU.mult, op1=ALU.add),
          [psk[7]], rstd2.k)
        A(lambda e: e.activation(out=rstd2.ap, in_=rstd2.ap, func=AF.Sqrt), rstd2.k, rstd2.k)
        V(lambda e: e.reciprocal(out=rstd2.ap, in_=rstd2.ap), rstd2.k, rstd2.k)
        for dt in range(DT):
            s = dt % 2
            V(lambda e, dt=dt, s=s, tt=tt: e.scalar_tensor_tensor(out=sg.ap[:, s, :], in0=h.ap[:, dt, tt * 512:(tt + 1) * 512],
                                                                  scalar=gf.ap[:, dt:dt + 1], in1=rstd2.ap, op0=ALU.mult, op1=ALU.mult),
              [hk(dt, tt), rstd2.k[0], gf.k[0]], [sg.k[s]])
            finals.append(dma("sp", yT[dt * 128:(dt + 1) * 128, tt * 512:(tt + 1) * 512], sg.ap[:, s, :], [sg.k[s]],
                              [dkey("yT")], "out_y%d" % s))
    fw = {}
    for d in finals + dbg_finals + [dk["carry_out"].last_w]:
        fw[d[1]] = max(fw.get(d[1], 0), d[2])
    P.emit([("dma", n, v) for n, v in fw.items()])
    nc._names = P.names
    return nc


_CACHE = {}


def _consts():
    c = np.zeros((128, 512), np.float32)
    c[:, 0:128] = np.eye(128, dtype=np.float32)
    rows = np.arange(128)
    gi = (rows // 16) % 2
    c[:, 128] = (gi == 0)
    c[:, 129] = (gi == 1)
    c[:, 130:258] = 1.0
    return c


def _cnt_tab(first_half):
    t = np.zeros((128, 4, 16), np.float32)
    for g, w in enumerate((2, 4, 8, 16)):
        for i in range(16):
            t[:, g, i] = 1.0 / (min(i + 1, w) if first_half else w)
    return t.reshape(128, 64)


WNAMES = ["norm_mix", "w_in", "w_pool", "pool_scale", "lam_re", "lam_im", "log_dt", "b_re", "b_im", "c_re", "c_im",
          "d_skip", "w_glu", "b_glu", "w_out", "norm_ffn", "w_gate", "w_up", "w_down"]


MODE = "fused"


def _common_maps(x, nf):
    consts = _consts()
    maps = []
    for r in range(8):
        b, half = r // 2, r % 2
        m = {}
        m["xT"] = np.ascontiguousarray(x[b, half * NTOK:(half + 1) * NTOK, :].T)
        m["norm_final"] = nf
        m["consts"] = consts
        m["cnt_tab"] = _cnt_tab(half == 0)
        s = np.zeros((128, 8), np.float32)
        if half == 1:
            s[:, r - 1] = 1.0
        m["sel"] = s
        maps.append(m)
    return maps


def kernel(**inputs):
    x = np.asarray(inputs["x"], np.float32)
    W = {k: np.ascontiguousarray(np.asarray(inputs[k], np.float32)) for k in WNAMES}
    nf = np.ascontiguousarray(np.asarray(inputs["norm_final"], np.float32))
    out = np.empty((4, 4096, D), np.float32)
    base = _common_maps(x, nf)
    if MODE == "fused":
        key = ("fused",)
        if key not in _CACHE:
            _CACHE[key] = build(DEPTH, True)
        nc = _CACHE[key]
        in_maps = []
        for r in range(8):
            m = dict(base[r])
            m.update(W)
            m["carry_in"] = np.zeros((DEPTH, 128, CARRY_W), np.float32)
            in_maps.append(m)
        res = run_bass_kernel_spmd(nc, in_maps, core_ids=list(range(8)))
        for r in range(8):
            b, half = r // 2, r % 2
            out[b, half * NTOK:(half + 1) * NTOK, :] = np.asarray(res.results[r]["yT"]).T
        return out
    key = ("layer",)
    if key not in _CACHE:
        _CACHE[key] = build(1, False)
    nc = _CACHE[key]
    Wl = [{k: np.ascontiguousarray(W[k][l:l + 1]) for k in WNAMES} for l in range(DEPTH)]
    hT = [base[r]["xT"] for r in range(8)]
    carry = [np.zeros((1, 128, CARRY_W), np.float32) for _ in range(8)]
    for step in range(DEPTH + 1):
        in_maps = []
        lay = []
        for r in range(8):
            half = r % 2
            l = step if half == 0 else step - 1
            lay.append(l)
            lw = min(max(l, 0), DEPTH - 1)
            m = dict(base[r])
            m.update(Wl[lw])
            m["xT"] = hT[r]
            m["carry_in"] = carry[r]
            in_maps.append(m)
        res = run_bass_kernel_spmd(nc, in_maps, core_ids=list(range(8)))
        for r in range(8):
            l = lay[r]
            if l < 0 or l >= DEPTH:
                continue
            hT[r] = np.ascontiguousarray(np.asarray(res.results[r]["hT_out"], np.float32))
            if r % 2 == 0:
                carry[r + 1] = np.ascontiguousarray(np.asarray(res.results[r]["carry_out"], np.float32))
            if l == DEPTH - 1:
                b, half = r // 2, r % 2
                out[b, half * NTOK:(half + 1) * NTOK, :] = np.asarray(res.results[r]["yT"]).T
    return out
```

```python
import math
import numpy as np
import concourse.bass as bass
import concourse.mybir as mybir
from concourse.bass_utils import run_bass_kernel_spmd

F32 = mybir.dt.float32
BF16 = mybir.dt.bfloat16
ALU = mybir.AluOpType
AF = mybir.ActivationFunctionType

DEPTH = 4
D = 1024
DT = 8
NTOK = 2048
TT = 4
DFF = 2816
FT = 22
T1 = 16
NC1 = 128
NPAIR = 16
EPS = 1e-6
SBUF_BASE = 16512
SBUF_LIMIT = 229376
CARRY_W = 2 * NPAIR + 4 * 16
TWO_PI = 2.0 * math.pi


class Key:
    __slots__ = ("space", "lo", "hi", "name", "last_w", "readers", "ovl")

    def __init__(self, space, lo, hi, name):
        self.space, self.lo, self.hi, self.name = space, lo, hi, name
        self.last_w = None
        self.readers = []
        self.ovl = None


class Prog:
    ENG = ("pe", "act", "dve", "pool", "sp")

    def __init__(self, nc):
        self.nc = nc
        self.ops = {e: [] for e in self.ENG}
        self.keys = []
        self.dma_cnt = {}
        self.dma_sems = []
        self.names = {}

    def key(self, space, lo, hi, name):
        k = Key(space, lo, hi, name)
        self.keys.append(k)
        return k

    def _ovl(self, k):
        if k.ovl is None or k.ovl[0] != len(self.keys):
            lst = [o for o in self.keys if o.space == k.space and o.lo < k.hi and k.lo < o.hi]
            k.ovl = (len(self.keys), lst)
        return k.ovl[1]

    def op(self, eng, fn, reads=(), writes=(), dma=None, inc=16):
        deps = set()
        for k in reads:
            for o in self._ovl(k):
                if o.last_w is not None:
                    deps.add(o.last_w)
        for k in writes:
            for o in self._ovl(k):
                if o.last_w is not None:
                    deps.add(o.last_w)
                for r in o.readers:
                    deps.add(r)
        deps = set(("dma", d[1], self.dma_cnt[d[1]]) if d[0] == "dma" else d for d in deps)
        idx = len(self.ops[eng])
        if dma is not None:
            self.dma_cnt[dma] = self.dma_cnt.get(dma, 0) + inc
            me = ("dma", dma, self.dma_cnt[dma])
        else:
            me = ("eng", eng, idx)
        import sys as _sys
        fr = _sys._getframe(1)
        while fr.f_code.co_name in ("V", "A", "G", "T", "dma", "cmul", "cplx_axpy_step"):
            fr = fr.f_back
        rec = dict(fn=fn, deps=deps, dma=dma, me=me, marked=False, inc=inc, line=fr.f_lineno)
        self.ops[eng].append(rec)
        for k in writes:
            for o in self._ovl(k):
                if o is not k:
                    o.readers = []
            k.last_w = me
            k.readers = []
        for k in reads:
            if me[0] == "eng":
                k.readers = [r for r in k.readers if not (r[0] == "eng" and r[1] == me[1])]
            k.readers.append(me)
        return me

    def emit(self, final_waits):
        nc = self.nc
        for e in self.ENG:
            for rec in self.ops[e]:
                for d in rec["deps"]:
                    if d[0] == "eng":
                        if d[1] == "pe" and e == "pe":
                            continue
                        self.ops[d[1]][d[2]]["marked"] = True
        for d in final_waits:
            if d[0] == "eng":
                self.ops[d[1]][d[2]]["marked"] = True
        cnt_at = {}
        for e in self.ENG:
            c = 0
            for i, rec in enumerate(self.ops[e]):
                if rec["marked"] and rec["dma"] is None:
                    c += 1
                cnt_at[(e, i)] = c
        import contextlib
        with contextlib.ExitStack() as st:
            esem = {e: st.enter_context(nc.semaphore("s_" + e)) for e in self.ENG}
            dsem = {n: st.enter_context(nc.semaphore("d_" + n)) for n in self.dma_cnt}
            block = st.enter_context(nc.Block())

            def run(e, eng):
                seen = {}
                for i, rec in enumerate(self.ops[e]):
                    need = {}
                    for d in rec["deps"]:
                        if d[0] == "eng":
                            if d[1] == "pe" and e == "pe":
                                continue
                            if d[1] == e and d[2] >= i:
                                continue
                            s, v = esem[d[1]], cnt_at[(d[1], d[2])]
                        else:
                            s, v = dsem[d[1]], d[2]
                        if v > need.get(s, (0, None))[0]:
                            need[s] = (v, s)
                    for s, (v, _) in need.items():
                        if seen.get(s, 0) < v:
                            eng.wait_ge(s, v)
                            seen[s] = v
                    ins = rec["fn"](eng)
                    try:
                        self.names[ins.ins.name] = rec["line"]
                    except Exception:
                        pass
                    if rec["dma"] is not None:
                        ins.then_inc(dsem[rec["dma"]], rec["inc"])
                    elif rec["marked"]:
                        ins.then_inc(esem[e], 1)
                if e == "sp":
                    for d in final_waits:
                        if d[0] == "eng":
                            eng.wait_ge(esem[d[1]], cnt_at[(d[1], d[2])])
                        else:
                            eng.wait_ge(dsem[d[1]], d[2])

            @block.tensor
            def _(eng):
                run("pe", eng)

            @block.scalar
            def _(eng):
                run("act", eng)

            @block.vector
            def _(eng):
                run("dve", eng)

            @block.gpsimd
            def _(eng):
                run("pool", eng)

            @block.sync
            def _(eng):
                run("sp", eng)


class Buf:
    def __init__(self, P, name, shape, dtype, off, nkeys=1):
        self.P = P
        esz = 2 if dtype == BF16 else 4
        n = 1
        for s in shape:
            n *= s
        self.nbytes = n * esz
        self.off = off
        self.t = P.nc.alloc_sbuf_tensor_at(name, [128] + list(shape), dtype, offset=SBUF_BASE + off)
        self.ap = self.t.ap()
        assert SBUF_BASE + off + self.nbytes <= SBUF_LIMIT, name
        assert self.nbytes % nkeys == 0
        step = self.nbytes // nkeys
        self.k = [P.key("sb", off + i * step, off + (i + 1) * step, "%s.%d" % (name, i)) for i in range(nkeys)]

    @property
    def end(self):
        return self.off + self.nbytes


def build(nl, use_ag, debug=False):
    nc = bass.Bass("TRN2", target_bir_lowering=False)
    P = Prog(nc)

    def din(name, shape):
        return nc.dram_tensor(name, list(shape), F32, kind="ExternalInput").ap()

    xT = din("xT", [D, NTOK])
    norm_mix = din("norm_mix", [nl, D])
    w_in = din("w_in", [nl, D, D])
    w_pool = din("w_pool", [nl, 4, 128, 128])
    pool_scale = din("pool_scale", [nl, 512])
    lam_re = din("lam_re", [nl, 32, 64])
    lam_im = din("lam_im", [nl, 32, 64])
    log_dt = din("log_dt", [nl, 32])
    b_re = din("b_re", [nl, 32, 64, 16])
    b_im = din("b_im", [nl, 32, 64, 16])
    c_re = din("c_re", [nl, 32, 16, 64])
    c_im = din("c_im", [nl, 32, 16, 64])
    d_skip = din("d_skip", [nl, 512])
    w_glu = din("w_glu", [nl, 512, 512])
    b_glu = din("b_glu", [nl, 512])
    w_out = din("w_out", [nl, D, D])
    norm_ffn = din("norm_ffn", [nl, D])
    w_gate = din("w_gate", [nl, D, DFF])
    w_up = din("w_up", [nl, D, DFF])
    w_down = din("w_down", [nl, DFF, D])
    norm_final = din("norm_final", [D])
    consts = din("consts", [128, 512])
    cnt_tab = din("cnt_tab", [128, 64])
    sel = din("sel", [128, 8])
    carry_in = din("carry_in", [nl, 128, CARRY_W])
    yT = nc.dram_tensor("yT", [D, NTOK], F32, kind="ExternalOutput").ap()
    hT_out = nc.dram_tensor("hT_out", [D, NTOK], F32, kind="ExternalOutput").ap()
    carry_out = nc.dram_tensor("carry_out", [nl, 128, CARRY_W], F32, kind="ExternalOutput").ap()
    if use_ag:
        cc_src = [nc.dram_tensor("cc_src%d" % l, [128, CARRY_W], F32) for l in range(nl)]
        cc_dst = [nc.dram_tensor("cc_dst%d" % l, [8 * 128, CARRY_W], F32) for l in range(nl)]
    dk = {}

    def dkey(name):
        if name not in dk:
            dk[name] = P.key("dram", len(dk), len(dk) + 1, name)
        return dk[name]

    dbg_finals = []

    def dbg(name, buf):
        if not debug:
            return
        shp = list(buf.ap.shape)
        t = nc.dram_tensor("dbg_" + name, shp, buf.ap.dtype, kind="ExternalOutput").ap()
        dbg_finals.append(P.op("sp", lambda e: e.dma_start(out=t, in_=buf.ap), buf.k, [dkey("dbg_" + name)], dma="dbg_" + name))

    off = [0]

    def alloc(name, shape, dtype, nkeys=1, at=None):
        o = off[0] if at is None else at
        b = Buf(P, name, shape, dtype, o, nkeys)
        if at is None:
            off[0] = (b.end + 31) // 32 * 32
        return b

    h = alloc("h", [DT, NTOK], F32, nkeys=DT * TT)
    cst = alloc("cst", [512], F32)
    cntb = alloc("cntb", [4, 16], F32)
    selb = alloc("selb", [8], F32)
    g1 = alloc("g1", [nl, DT], F32)
    g2 = alloc("g2", [nl, DT], F32)
    gf = alloc("gf", [DT], F32)
    pscale = alloc("pscale", [nl, 4], F32)
    bglu = alloc("bglu", [nl, 4], F32)
    dskp = alloc("dskp", [nl, 4], F32)
    ident_bf = alloc("ident_bf", [128], BF16)
    ones_bf = alloc("ones_bf", [128], BF16)
    zeros = alloc("zeros", [2, NC1], F32)
    lamn = alloc("lamn", [3, 128], F32)
    prm = alloc("prm", [24, NPAIR], F32)
    prmi = alloc("prmi", [NPAIR], mybir.dt.int32)
    pw1 = alloc("pw1", [2, NPAIR, 16], F32)
    pw2 = alloc("pw2", [2, NPAIR, 16], F32)
    pwt = alloc("pwt", [4, NPAIR, 8], F32)
    WB = alloc("WB", [4, 2, 128], BF16)
    WC = alloc("WC", [NPAIR, 2, 32], BF16)
    diagD = alloc("diagD", [4, 128], BF16)
    Xe = alloc("Xe", [2, NPAIR, 1 + NC1], F32, nkeys=2 * NPAIR)
    Fb = alloc("Fb", [2, NPAIR, 9], F32)
    l3t = alloc("l3t", [4, NPAIR], F32)
    sin_b = alloc("sin_b", [2, NPAIR], F32)
    cblk = alloc("cblk", [CARRY_W], F32)
    gath = alloc("gath", [8, CARRY_W], F32)
    persist_end = off[0]
    w_in_b = alloc("w_in_b", [DT, D], BF16, nkeys=DT)
    w_glu_b = alloc("w_glu_b", [4, 512], BF16)
    w_pool_b = alloc("w_pool_b", [4, 128], BF16)
    common_end = off[0]
    regA = off[0]
    sq = alloc("sq", [2, 512], BF16, nkeys=2)
    rstd = alloc("rstd", [2, 512], F32, nkeys=2)
    xn1 = alloc("xn1", [2, DT, 512], BF16, nkeys=2 * DT)
    regA_end = off[0]
    off[0] = regA
    sring = alloc("sring", [4, 2, 2, NC1], F32, nkeys=8)
    stmp = alloc("stmp", [4, 2, NC1], F32, nkeys=4)
    l2t = alloc("l2t", [4, NPAIR * 16], F32)
    off[0] = max(off[0], regA_end)
    poolreg = off[0]
    upool = alloc("upool", [4, 16 + 512], F32, nkeys=4)
    upool0 = alloc("upool0", [4, 16 + 512], F32, nkeys=4)
    ps_s = alloc("ps_s", [2, 528], F32, nkeys=2)
    dpool = alloc("dpool", [4, 512], BF16, nkeys=4)
    poolreg_end = off[0]
    off[0] = poolreg
    bnat = alloc("bnat", [2, NPAIR, 32], F32)
    bbar = alloc("bbar", [2, NPAIR, 32], F32)
    btmp = alloc("btmp", [2, NPAIR, 32], F32)
    cnat = alloc("cnat", [4, 2, 64], F32)
    cin = alloc("cin", [4, 2, 128], F32)
    assert off[0] <= poolreg_end
    off[0] = poolreg
    sbf = alloc("sbf", [2, 2, T1, NC1], BF16, nkeys=2 * 2 * 4)
    gt = alloc("gt", [1, 2, 512], F32, nkeys=2)
    assert off[0] <= poolreg_end
    off[0] = poolreg_end
    ussm = alloc("ussm", [4, T1, NC1], BF16, nkeys=16)
    ymp = alloc("ymp", [4, T1, NC1], BF16, nkeys=16)
    w_out_b = alloc("w_out_b", [DT, D], BF16, nkeys=DT, at=w_in_b.off)
    mixer_end = off[0]
    off[0] = w_glu_b.off
    sq2 = alloc("sq2", [2, 512], BF16, nkeys=2)
    rstd2 = alloc("rstd2", [512], F32)
    xn2 = alloc("xn2", [DT, 1024], BF16, nkeys=DT * 2)
    act = alloc("act", [FT, 1024], BF16, nkeys=FT * 2)
    sg = alloc("sg", [2, 512], F32, nkeys=2)
    wgu = alloc("wgu", [2, 2, DT, 128], BF16, nkeys=2)
    wd = alloc("wd", [2, FT, 128], BF16, nkeys=2)
    ffn_end = off[0]
    assert SBUF_BASE + max(mixer_end, ffn_end) <= SBUF_LIMIT, (mixer_end, ffn_end)

    ps_t = nc.alloc_psum_tensor("ps", [128, 8, 512], F32)
    ps = ps_t.ap()
    psk = [P.key("ps", b, b + 1, "ps%d" % b) for b in range(8)]

    ident = cst.ap[:, 0:128]
    m_gi = [cst.ap[:, 128:129], cst.ap[:, 129:130]]

    def V(fn, r, w):
        return P.op("dve", fn, r, w)

    def A(fn, r, w):
        return P.op("act", fn, r, w)

    def G(fn, r, w):
        return P.op("pool", fn, r, w)

    def T(fn, r, w):
        return P.op("pe", fn, r, w)

    def dma(q, out, in_, r, w, sem, **kw):
        return P.op(q, lambda e: e.dma_start(out=out, in_=in_, **kw), r, w, dma=sem)

    def hk(dt, tt):
        return h.k[dt * TT + tt]

    def bc(ap2, shape):
        return ap2.broadcast_to(shape)

    dma("sp", cst.ap, consts, [], cst.k, "cst")
    dma("sp", cntb.ap, cnt_tab.rearrange("p (g t) -> p g t", g=4), [], cntb.k, "cst")
    dma("sp", selb.ap, sel, [], selb.k, "cst")
    for dt in range(DT):
        dma("sp", h.ap[:, dt, :], xT[dt * 128:(dt + 1) * 128, :], [], [hk(dt, t) for t in range(TT)], "hload")

    def small_T(dst, src, n):
        dma("act", dst, src.rearrange("l (t c) -> c l t", c=128), [], [], "cst",
            allow_slow_non_contiguous=True)

    small_T(g1.ap, norm_mix, DT)
    small_T(g2.ap, norm_ffn, DT)
    small_T(pscale.ap, pool_scale, 4)
    small_T(bglu.ap, b_glu, 4)
    small_T(dskp.ap, d_skip, 4)
    P.op("act", lambda e: e.dma_start(out=gf.ap, in_=norm_final.rearrange("(t c) -> c t", c=128),
                                      allow_slow_non_contiguous=True),
         [], g1.k + g2.k + gf.k + pscale.k + bglu.k + dskp.k, dma="cst")
    V(lambda e: e.tensor_copy(out=ident_bf.ap, in_=ident), cst.k, ident_bf.k)
    V(lambda e: e.memset(ones_bf.ap, 1.0), [], ones_bf.k)
    V(lambda e: e.memset(zeros.ap, 0.0), [], zeros.k)

    def norm_tile(tt, gain_ap, sqb, rstd_ap, rstd_k, xn_of, xn_keys, bank):
        for dt in range(DT):
            s = dt % 2
            A(lambda e, dt=dt, s=s: e.activation(out=sqb.ap[:, s, :], in_=h.ap[:, dt, tt * 512:(tt + 1) * 512],
                                                 func=AF.Square),
              [hk(dt, tt)], [sqb.k[s]])
            T(lambda e, dt=dt, s=s: e.matmul(ps[:, bank, :], lhsT=ones_bf.ap, rhs=sqb.ap[:, s, :],
                                             start=(dt == 0), stop=(dt == DT - 1)),
              [sqb.k[s], ones_bf.k[0]], [psk[bank]])
        V(lambda e: e.tensor_scalar(out=rstd_ap, in0=ps[:, bank, :], scalar1=1.0 / D, scalar2=EPS,
                                    op0=ALU.mult, op1=ALU.add),
          [psk[bank]], rstd_k)
        A(lambda e: e.activation(out=rstd_ap, in_=rstd_ap, func=AF.Sqrt), rstd_k, rstd_k)
        V(lambda e: e.reciprocal(out=rstd_ap, in_=rstd_ap), rstd_k, rstd_k)
        for dt in range(DT):
            V(lambda e, dt=dt: e.scalar_tensor_tensor(out=xn_of(dt), in0=h.ap[:, dt, tt * 512:(tt + 1) * 512],
                                                      scalar=gain_ap[:, dt:dt + 1], in1=rstd_ap,
                                                      op0=ALU.mult, op1=ALU.mult),
              [hk(dt, tt), g1.k[0]] + rstd_k, [xn_keys[dt]])

    POOLW = (2, 4, 8, 16)

    def pool_tile(l, tt, ub, bank0):
        for g in range(4):
            w = POOLW[g]
            u = ub.ap[:, g, :]
            cur, curk = u, ub.k[g]
            sh = 1
            si = 0
            while sh < w:
                dst = ps_s.ap[:, si, :]
                V(lambda e, cur=cur, dst=dst, sh=sh: e.tensor_add(out=dst[:, sh:528], in0=cur[:, sh:528], in1=cur[:, 0:528 - sh]),
                  [curk], [ps_s.k[si]])
                cur, curk = dst, ps_s.k[si]
                si ^= 1
                sh *= 2
            V(lambda e, cur=cur, u=u, g=g, w=w: e.scalar_tensor_tensor(out=dpool.ap[:, g, :], in0=cur[:, 16:528], scalar=1.0 / w,
                                                                        in1=u[:, 16:528], op0=ALU.mult, op1=ALU.subtract),
              [curk, ub.k[g]], [dpool.k[g]])
            if tt == 0:
                dst = ps_s.ap[:, si, :]
                V(lambda e, cur=cur, dst=dst, g=g: e.tensor_mul(out=dst[:, 0:16], in0=cur[:, 16:32], in1=cntb.ap[:, g, :]),
                  [curk, cntb.k[0]], [ps_s.k[si]])
                V(lambda e, dst=dst, u=u, g=g: e.tensor_sub(out=dpool.ap[:, g, 0:16], in0=dst[:, 0:16], in1=u[:, 16:32]),
                  [ps_s.k[si], ub.k[g]], [dpool.k[g]])
            bank = bank0 + (g % 2)
            T(lambda e, g=g, bank=bank: e.matmul(ps[:, bank, :], lhsT=w_pool_b.ap[:, g, :], rhs=dpool.ap[:, g, :],
                                                 start=True, stop=True),
              [dpool.k[g], w_pool_b.k[0]], [psk[bank]])
            A(lambda e, g=g, bank=bank: e.activation(out=ymp.ap[:, g, :, tt * 32:(tt + 1) * 32],
                                                     in_=ps[:, bank, :].rearrange("p (c j) -> p j c", j=T1),
                                                     func=AF.Copy, scale=pscale.ap[:, l, g:g + 1]),
              [psk[bank], pscale.k[0]], [ymp.k[g * 4 + b] for b in range(4)])

    def cmul(eng_op, out_r, out_i, a_r, a_i, b_r, b_i, t1, t2, rk, wk, tk):
        eng_op(lambda e: e.tensor_mul(out=t1, in0=a_r, in1=b_r), rk, tk)
        eng_op(lambda e: e.tensor_mul(out=t2, in0=a_i, in1=b_i), rk, tk)
        eng_op(lambda e: e.tensor_sub(out=out_r, in0=t1, in1=t2), tk, wk)
        eng_op(lambda e: e.tensor_mul(out=t1, in0=a_r, in1=b_i), rk, tk)
        eng_op(lambda e: e.tensor_mul(out=t2, in0=a_i, in1=b_r), rk, tk)
        eng_op(lambda e: e.tensor_add(out=out_i, in0=t1, in1=t2), tk, wk)

    def ssm_setup(l):
        S = lambda i: prm.ap[:, i, :]
        pk = prm.k
        for i, src in enumerate((lam_re, lam_im)):
            dma("sp", lamn.ap[0:16, i, :], src[l].rearrange("(pr gi) p -> pr (gi p)", gi=2), [], lamn.k, "lam")
        dma("sp", lamn.ap[0:1, 2, 0:32], log_dt[l].unsqueeze(0), [], lamn.k, "lam")
        T(lambda e: e.matmul(ps[:, 0, 32:64], lhsT=cst.ap[0:1, 130:258], rhs=lamn.ap[0:1, 2, 0:32], start=True, stop=True),
          lamn.k + cst.k, [psk[0]])
        for i in range(2):
            T(lambda e, i=i: e.transpose(ps[:, 0, i * 16:(i + 1) * 16], lamn.ap[0:16, i, :], ident[0:16, 0:16]),
              lamn.k + cst.k, [psk[0]])
        V(lambda e: e.tensor_copy(out=prm.ap[:, 0:2, :], in_=ps[:, 0, 0:32].rearrange("p (a b) -> p a b", a=2)),
          [psk[0]], pk)
        for gi in range(2):
            V(lambda e, gi=gi: e.tensor_copy(out=prm.ap[gi * 64:(gi + 1) * 64, 2, :],
                                             in_=ps[gi * 64:(gi + 1) * 64, 0, 32:64].rearrange("p (pr gi) -> p pr gi", gi=2)[:, :, gi]),
              [psk[0]], pk)
        lr, li, dtt = S(0), S(1), S(2)
        A(lambda e: e.activation(out=dtt, in_=dtt, func=AF.Exp), pk, pk)
        V(lambda e: e.tensor_mul(out=S(3), in0=lr, in1=dtt), pk, pk)
        A(lambda e: e.activation(out=S(3), in_=S(3), func=AF.Exp), pk, pk)
        V(lambda e: e.tensor_mul(out=S(4), in0=li, in1=dtt), pk, pk)
        V(lambda e: e.tensor_scalar(out=S(5), in0=S(4), scalar1=0.125, scalar2=3.1415, op0=ALU.mult, op1=ALU.min), pk, pk)
        V(lambda e: e.tensor_scalar(out=S(6), in0=S(5), scalar1=-1.0, scalar2=0.5 * math.pi, op0=ALU.mult, op1=ALU.add), pk, pk)
        A(lambda e: e.activation(out=S(5), in_=S(5), func=AF.Sin), pk, pk)
        A(lambda e: e.activation(out=S(6), in_=S(6), func=AF.Sin), pk, pk)
        for _ in range(3):
            V(lambda e: e.tensor_mul(out=S(10), in0=S(5), in1=S(6)), pk, pk)
            V(lambda e: e.tensor_mul(out=S(11), in0=S(5), in1=S(5)), pk, pk)
            V(lambda e: e.tensor_mul(out=S(6), in0=S(6), in1=S(6)), pk, pk)
            V(lambda e: e.tensor_sub(out=S(6), in0=S(6), in1=S(11)), pk, pk)
            V(lambda e: e.tensor_scalar(out=S(5), in0=S(10), scalar1=2.0, scalar2=None, op0=ALU.mult), pk, pk)
        ar, ai, nai = S(7), S(8), S(9)
        V(lambda e: e.tensor_mul(out=ar, in0=S(6), in1=S(3)), pk, pk)
        V(lambda e: e.tensor_mul(out=ai, in0=S(5), in1=S(3)), pk, pk)
        V(lambda e: e.scalar_tensor_tensor(out=nai, in0=S(5), scalar=-1.0, in1=S(3), op0=ALU.mult, op1=ALU.mult), pk, pk)
        V(lambda e: e.tensor_mul(out=S(10), in0=lr, in1=lr), pk, pk)
        V(lambda e: e.tensor_mul(out=S(11), in0=li, in1=li), pk, pk)
        V(lambda e: e.tensor_add(out=S(10), in0=S(10), in1=S(11)), pk, pk)
        V(lambda e: e.reciprocal(out=S(10), in_=S(10)), pk, pk)
        V(lambda e: e.tensor_scalar_add(out=S(11), in0=ar, scalar1=-1.0), pk, pk)
        V(lambda e: e.tensor_mul(out=S(12), in0=S(11), in1=lr), pk, pk)
        V(lambda e: e.tensor_mul(out=S(13), in0=ai, in1=li), pk, pk)
        V(lambda e: e.tensor_add(out=S(12), in0=S(12), in1=S(13)), pk, pk)
        V(lambda e: e.tensor_mul(out=S(12), in0=S(12), in1=S(10)), pk, pk)
        V(lambda e: e.tensor_mul(out=S(13), in0=ai, in1=lr), pk, pk)
        V(lambda e: e.tensor_mul(out=S(14), in0=S(11), in1=li), pk, pk)
        V(lambda e: e.tensor_sub(out=S(13), in0=S(13), in1=S(14)), pk, pk)
        V(lambda e: e.tensor_mul(out=S(13), in0=S(13), in1=S(10)), pk, pk)
        V(lambda e: e.memset(bnat.ap, 0.0), [], bnat.k)
        for part, src in enumerate((b_re, b_im)):
            for gi in range(2):
                dma("sp", bnat.ap[gi * 64:(gi + 1) * 64, part, :, gi * 16:(gi + 1) * 16],
                    src[l].rearrange("(pr gi) p h -> gi p pr h", gi=2)[gi],
                    [], bnat.k, "bload", allow_slow_non_contiguous=True)
        sh3 = [128, NPAIR, 32]
        cr = S(12).unsqueeze(2).broadcast_to(sh3)
        ci = S(13).unsqueeze(2).broadcast_to(sh3)
        cmul(V, bbar.ap[:, 0], bbar.ap[:, 1], cr, ci, bnat.ap[:, 0], bnat.ap[:, 1], btmp.ap[:, 0], btmp.ap[:, 1],
             pk + bnat.k, bbar.k, btmp.k)
        for t4 in range(4):
            for part in range(2):
                bank = 1 + (t4 * 2 + part) % 2
                T(lambda e, t4=t4, part=part, bank=bank: e.transpose(
                    ps[:, bank, 0:128], bbar.ap[:, part, t4 * 4:(t4 + 1) * 4, :].rearrange("p a b -> p (a b)"), ident),
                  bbar.k + cst.k, [psk[bank]])
                A(lambda e, t4=t4, part=part, bank=bank: e.copy(out=WB.ap[:, t4, part, :], in_=ps[:, bank, 0:128]),
                  [psk[bank]], WB.k)
        for part, src in enumerate((c_re, c_im)):
            dma("sp", cnat.ap[:, :, part, :], src[l].rearrange("(t r) h p -> (r h) t p", t=4), [], cnat.k, "cload")
        for gi in range(2):
            V(lambda e, gi=gi: e.tensor_scalar(out=cin.ap[:, :, :, gi * 64:(gi + 1) * 64], in0=cnat.ap,
                                               scalar1=m_gi[gi], scalar2=None, op0=ALU.mult),
              cnat.k + cst.k, cin.k)
        for t4 in range(4):
            for part in range(2):
                bank = 1 + (t4 * 2 + part) % 2
                T(lambda e, t4=t4, part=part, bank=bank: e.transpose(ps[:, bank, 0:128], cin.ap[:, t4, part, :], ident),
                  cin.k + cst.k, [psk[bank]])
                sc = 1.0 if part == 0 else -1.0
                A(lambda e, t4=t4, part=part, bank=bank, sc=sc: e.activation(
                    out=WC.ap[:, t4 * 4:(t4 + 1) * 4, part, :], in_=ps[:, bank, 0:128].rearrange("p (a b) -> p a b", a=4),
                    func=AF.Copy, scale=sc),
                  [psk[bank]], WC.k)
        for t4 in range(4):
            V(lambda e, t4=t4: e.tensor_scalar(out=diagD.ap[:, t4, :], in0=ident, scalar1=dskp.ap[:, l, t4:t4 + 1],
                                               scalar2=None, op0=ALU.mult),
              cst.k + dskp.k, diagD.k)
        def powers(pw, base_r, base_i, rk):
            V(lambda e: e.tensor_copy(out=pw.ap[:, 0, :, 0], in_=base_r), rk, pw.k)
            V(lambda e: e.tensor_copy(out=pw.ap[:, 1, :, 0], in_=base_i), rk, pw.k)
            n = 1
            while n < 16:
                shp = [128, NPAIR, n]
                br = pw.ap[:, 0, :, n - 1:n].broadcast_to(shp)
                bi = pw.ap[:, 1, :, n - 1:n].broadcast_to(shp)
                cmul(V, pw.ap[:, 0, :, n:2 * n], pw.ap[:, 1, :, n:2 * n], pw.ap[:, 0, :, 0:n], pw.ap[:, 1, :, 0:n], br, bi,
                     pwt.ap[:, 0, :, 0:n], pwt.ap[:, 1, :, 0:n], pw.k, pw.k, pwt.k)
                n *= 2
        powers(pw1, ar, ai, pk)
        powers(pw2, pw1.ap[:, 0, :, 15], pw1.ap[:, 1, :, 15], pw1.k)

    def ssm_group(prs, final):
        for blk in range(4):
            for L, pr in enumerate(prs):
                t4, q = pr // 4, pr % 4
                zb = 2 * L
                for part in range(2):
                    T(lambda e, part=part, blk=blk, zb=zb, t4=t4, q=q: e.matmul(
                        ps[:, zb + part, :], lhsT=WB.ap[32 * q:32 * q + 32, t4, part, :],
                        rhs=ussm.ap[32 * q:32 * q + 32, t4, 4 * blk:4 * blk + 4, :].rearrange("p a b -> p (a b)"),
                        start=True, stop=True, tile_position=(32 * q, 0)),
                      [ussm.k[t4 * 4 + blk], WB.k[0]], [psk[zb + part]])
            for jj in range(4):
                j = 4 * blk + jj
                ctx = []
                for L, pr in enumerate(prs):
                    zb = 2 * L
                    c = dict(pr=pr, zb=zb, bu=ps[:, zb:zb + 2, jj * 128:(jj + 1) * 128],
                             ar=prm.ap[:, 7, pr:pr + 1], ai=prm.ap[:, 8, pr:pr + 1], nai=prm.ap[:, 9, pr:pr + 1])
                    if j == 0:
                        if final:
                            c["prev"], c["prevk"] = Xe.ap[:, :, pr, 0:NC1], [Xe.k[pr], Xe.k[NPAIR + pr]]
                        else:
                            c["prev"], c["prevk"] = zeros.ap, zeros.k
                    else:
                        c["prev"], c["prevk"] = sring.ap[:, L, (j - 1) % 2], [sring.k[L * 2 + (j - 1) % 2]]
                    if (not final) and j == T1 - 1:
                        c["cur"], c["curk"] = Xe.ap[:, :, pr, 1:1 + NC1], [Xe.k[pr], Xe.k[NPAIR + pr]]
                    else:
                        c["cur"], c["curk"] = sring.ap[:, L, j % 2], [sring.k[L * 2 + j % 2]]
                    c["tm"], c["tmk"] = stmp.ap[:, L], [stmp.k[L]]
                    ctx.append(c)
                for c in ctx:
                    V(lambda e, c=c: e.scalar_tensor_tensor(out=c["tm"], in0=c["prev"], scalar=c["ar"], in1=c["bu"],
                                                            op0=ALU.mult, op1=ALU.add),
                      c["prevk"] + [psk[c["zb"]], psk[c["zb"] + 1], prm.k[0]], c["tmk"])
                for c in ctx:
                    V(lambda e, c=c: e.scalar_tensor_tensor(out=c["cur"][:, 0], in0=c["prev"][:, 1], scalar=c["nai"], in1=c["tm"][:, 0],
                                                            op0=ALU.mult, op1=ALU.add),
                      c["prevk"] + c["tmk"], c["curk"])
                for c in ctx:
                    V(lambda e, c=c: e.scalar_tensor_tensor(out=c["cur"][:, 1], in0=c["prev"][:, 0], scalar=c["ai"], in1=c["tm"][:, 1],
                                                            op0=ALU.mult, op1=ALU.add),
                      c["prevk"] + c["tmk"], c["curk"])
                if final:
                    for L, c in enumerate(ctx):
                        A(lambda e, c=c, j=j, L=L: e.copy(out=sbf.ap[:, L, :, j, :], in_=c["cur"]),
                          c["curk"], [sbf.k[(L * 2 + part) * 4 + blk] for part in range(2)])
            if final:
                yb = 4 + blk
                for L, pr in enumerate(prs):
                    t4, q = pr // 4, pr % 4
                    rhs_u = ussm.ap[32 * q:32 * q + 32, t4, 4 * blk:4 * blk + 4, :].rearrange("p a b -> p (a b)")
                    for part in range(2):
                        T(lambda e, part=part, blk=blk, yb=yb, pr=pr, q=q, L=L: e.matmul(
                            ps[32 * q:32 * q + 32, yb, :], lhsT=WC.ap[:, pr, part, :],
                            rhs=sbf.ap[:, L, part, 4 * blk:4 * blk + 4, :].rearrange("p a b -> p (a b)"),
                            start=(part == 0), stop=False, tile_position=(0, 32 * q)),
                          [sbf.k[(L * 2 + part) * 4 + blk], WC.k[0]], [psk[yb]])
                    T(lambda e, yb=yb, rhs_u=rhs_u, t4=t4, q=q: e.matmul(
                        ps[32 * q:32 * q + 32, yb, :], lhsT=diagD.ap[32 * q:32 * q + 32, t4, 32 * q:32 * q + 32], rhs=rhs_u,
                        start=False, stop=True, tile_position=(32 * q, 32 * q)),
                      [ussm.k[t4 * 4 + blk], diagD.k[0]], [psk[yb]])

    def gelu_tile(t4):
        for blk in range(4):
            yb = 4 + blk
            y = ps[:, yb, :]
            t_a = gt.ap[:, 0, 0, :]
            t_b = gt.ap[:, 0, 1, :]
            tk = [gt.k[0], gt.k[1]]
            A(lambda e, y=y, t_a=t_a: e.activation(out=t_a, in_=y, func=AF.Square), [psk[yb]], tk)
            V(lambda e, t_a=t_a: e.tensor_scalar(out=t_a, in0=t_a, scalar1=0.044715, scalar2=1.0, op0=ALU.mult, op1=ALU.add), tk, tk)
            V(lambda e, y=y, t_a=t_a: e.tensor_mul(out=t_a, in0=t_a, in1=y), tk + [psk[yb]], tk)
            A(lambda e, t_a=t_a, t_b=t_b: e.activation(out=t_b, in_=t_a, func=AF.Sigmoid, scale=2.0 * math.sqrt(2.0 / math.pi)), tk, tk)
            V(lambda e, y=y, t_b=t_b, blk=blk: e.tensor_mul(out=ussm.ap[:, t4, 4 * blk:4 * blk + 4, :].rearrange("p a b -> p (a b)"),
                                                            in0=t_b, in1=y),
              tk + [psk[yb]], [ussm.k[t4 * 4 + blk]])

    def cplx_axpy_step(cur_r, cur_i, prev_r, prev_i, m_r, m_i, shape, rk, wk):
        t = [l2t.ap[:, i, 0:shape[0] * shape[1]].rearrange("p (a b) -> p a b", a=shape[0]) for i in range(4)]
        tk = l2t.k
        V(lambda e: e.tensor_mul(out=t[0], in0=m_r, in1=prev_r), rk, tk)
        V(lambda e: e.tensor_mul(out=t[1], in0=m_i, in1=prev_i), rk, tk)
        V(lambda e: e.tensor_sub(out=t[0], in0=t[0], in1=t[1]), tk, tk)
        V(lambda e: e.tensor_add(out=cur_r, in0=cur_r, in1=t[0]), tk + wk, wk)
        V(lambda e: e.tensor_mul(out=t[2], in0=m_i, in1=prev_r), rk, tk)
        V(lambda e: e.tensor_mul(out=t[3], in0=m_r, in1=prev_i), rk, tk)
        V(lambda e: e.tensor_add(out=t[2], in0=t[2], in1=t[3]), tk, tk)
        V(lambda e: e.tensor_add(out=cur_i, in0=cur_i, in1=t[2]), tk + wk, wk)

    def level2():
        al_r = pw1.ap[:, 0, :, 15:16].broadcast_to([128, NPAIR, 8])
        al_i = pw1.ap[:, 1, :, 15:16].broadcast_to([128, NPAIR, 8])
        for j2 in range(1, 16):
            cr = Xe.ap[:, 0, :, 1 + j2:1 + NC1:16]
            ci = Xe.ap[:, 1, :, 1 + j2:1 + NC1:16]
            pr_ = Xe.ap[:, 0, :, j2:NC1:16]
            pi_ = Xe.ap[:, 1, :, j2:NC1:16]
            cplx_axpy_step(cr, ci, pr_, pi_, al_r, al_i, [NPAIR, 8], Xe.k + pw1.k, Xe.k)

    def level3(init_r, init_i, initk):
        V(lambda e: e.tensor_copy(out=Fb.ap[:, 0, :, 0], in_=init_r), initk, Fb.k)
        V(lambda e: e.tensor_copy(out=Fb.ap[:, 1, :, 0], in_=init_i), initk, Fb.k)
        a2r = pw2.ap[:, 0, :, 15]
        a2i = pw2.ap[:, 1, :, 15]
        t = [l3t.ap[:, i] for i in range(4)]
        tk = l3t.k
        for c2 in range(8):
            er = Xe.ap[:, 0, :, 16 * c2 + 16]
            ei = Xe.ap[:, 1, :, 16 * c2 + 16]
            fr, fi = Fb.ap[:, 0, :, c2], Fb.ap[:, 1, :, c2]
            nr, ni = Fb.ap[:, 0, :, c2 + 1], Fb.ap[:, 1, :, c2 + 1]
            rk = Fb.k + pw2.k + Xe.k
            V(lambda e, fr=fr: e.tensor_mul(out=t[0], in0=a2r, in1=fr), rk, tk)
            V(lambda e, fi=fi: e.tensor_mul(out=t[1], in0=a2i, in1=fi), rk, tk)
            V(lambda e: e.tensor_sub(out=t[0], in0=t[0], in1=t[1]), tk, tk)
            V(lambda e, nr=nr, er=er: e.tensor_add(out=nr, in0=t[0], in1=er), tk + rk, Fb.k)
            V(lambda e, fr=fr: e.tensor_mul(out=t[2], in0=a2i, in1=fr), rk, tk)
            V(lambda e, fi=fi: e.tensor_mul(out=t[3], in0=a2r, in1=fi), rk, tk)
            V(lambda e: e.tensor_add(out=t[2], in0=t[2], in1=t[3]), tk, tk)
            V(lambda e, ni=ni, ei=ei: e.tensor_add(out=ni, in0=t[2], in1=ei), tk + rk, Fb.k)

    def propagate():
        for c2 in range(8):
            cr = Xe.ap[:, 0, :, 1 + 16 * c2:17 + 16 * c2]
            ci = Xe.ap[:, 1, :, 1 + 16 * c2:17 + 16 * c2]
            fr = Fb.ap[:, 0, :, c2:c2 + 1].broadcast_to([128, NPAIR, 16])
            fi = Fb.ap[:, 1, :, c2:c2 + 1].broadcast_to([128, NPAIR, 16])
            cplx_axpy_step(cr, ci, fr, fi, pw2.ap[:, 0], pw2.ap[:, 1], [NPAIR, 16], Xe.k + pw2.k + Fb.k, Xe.k)
        V(lambda e: e.tensor_copy(out=Xe.ap[:, :, :, 0], in_=Fb.ap[:, :, :, 0]), Fb.k, Xe.k)

    def layer(l, last):
        for kt in range(DT):
            dma("pool", w_in_b.ap[:, kt, :], w_in[l, kt * 128:(kt + 1) * 128, :], [], [w_in_b.k[kt]], "w_in")
        dma("pool", w_pool_b.ap, w_pool[l].rearrange("g c d -> c g d"), [], w_pool_b.k, "w_pool")
        dma("pool", w_glu_b.ap, w_glu[l].rearrange("(kt p) n -> p kt n", p=128), [], w_glu_b.k, "w_glu")
        ssm_setup(l)
        if l == 0:
            for nm, b in (("prm", prm), ("pw1", pw1), ("pw2", pw2), ("WB", WB), ("WC", WC), ("diagD", diagD)):
                dbg(nm, b)
        def norm1(tt):
            sl = tt % 2
            norm_tile(tt, g1.ap[:, l, :], sq, rstd.ap[:, sl, :], [rstd.k[sl]], lambda dt, sl=sl: xn1.ap[:, sl, dt, :],
                      [xn1.k[sl * DT + dt] for dt in range(DT)], 7)

        norm1(0)
        for tt in range(TT):
            if tt + 1 < TT:
                norm1(tt + 1)
            sl = tt % 2
            ub = upool0 if tt == 0 else upool
            for n in (4, 5, 6, 7, 0, 1, 2, 3):
                bank = n % 4
                for kt in range(DT):
                    T(lambda e, n=n, kt=kt, bank=bank, sl=sl: e.matmul(ps[:, bank, :], lhsT=w_in_b.ap[:, kt, n * 128:(n + 1) * 128],
                                                                       rhs=xn1.ap[:, sl, kt, :], start=(kt == 0), stop=(kt == DT - 1)),
                      [w_in_b.k[kt], xn1.k[sl * DT + kt]], [psk[bank]])
                if n < 4:
                    A(lambda e, n=n, bank=bank, ub=ub: e.copy(out=ub.ap[:, n, 16:528], in_=ps[:, bank, :]), [psk[bank]], [ub.k[n]])
                else:
                    A(lambda e, n=n, bank=bank, tt=tt: e.copy(out=ussm.ap[:, n - 4, :, tt * 32:(tt + 1) * 32],
                                                              in_=ps[:, bank, :].rearrange("p (c j) -> p j c", j=T1)),
                      [psk[bank]], [ussm.k[(n - 4) * 4 + b] for b in range(4)])
            if tt == 1:
                V(lambda e: e.tensor_copy(out=upool.ap[:, :, 0:16], in_=upool0.ap[:, :, 512:528]), upool0.k, upool.k)
            if tt >= 1:
                pool_tile(l, tt, upool, 4)
                if tt < TT - 1:
                    V(lambda e: e.tensor_copy(out=ps_s.ap[:, 0, 0:64].rearrange("p (g t) -> p g t", g=4), in_=upool.ap[:, :, 512:528]),
                      upool.k, [ps_s.k[0]])
                    V(lambda e: e.tensor_copy(out=upool.ap[:, :, 0:16], in_=ps_s.ap[:, 0, 0:64].rearrange("p (g t) -> p g t", g=4)),
                      [ps_s.k[0]], upool.k)
        for kt in range(DT):
            dma("pool", w_out_b.ap[:, kt, :], w_out[l, kt * 128:(kt + 1) * 128, :], [], [w_out_b.k[kt]], "w_out")
        if l == 0:
            dbg("ussm", ussm)
        for t4 in range(4):
            ssm_group([t4 * 4 + q for q in range(4)], False)
        if l == 0:
            dbg("Xe1", Xe)
        level2()
        if l == 0:
            dbg("Xe1b", Xe)
        level3(zeros.ap[:, 0, 0:NPAIR], zeros.ap[:, 0, 0:NPAIR], zeros.k)
        V(lambda e: e.tensor_copy(out=cblk.ap[:, 0:32].rearrange("p (a b) -> p a b", a=2), in_=Fb.ap[:, :, :, 8]), Fb.k, cblk.k)
        V(lambda e: e.tensor_copy(out=cblk.ap[:, 32:96].rearrange("p (g t) -> p g t", g=4), in_=upool.ap[:, :, 512:528]),
          upool.k, cblk.k)
        dma("sp", carry_out[l], cblk.ap, cblk.k, [dkey("carry_out")], "carry_o")
        if use_ag:
            dma("pool", cc_src[l].ap(), cblk.ap, cblk.k, [dkey("cc_src%d" % l)], "cc_s%d" % l)
            P.op("pool", lambda e: e.collective_compute("AllGather", ALU.bypass, replica_groups=[list(range(8))],
                                                        ins=[cc_src[l].ap().opt()], outs=[cc_dst[l].ap().opt()]),
                 [dkey("cc_src%d" % l)], [dkey("cc_dst%d" % l)], dma="ccs%d" % l, inc=1)
            dma("pool", gath.ap, cc_dst[l].ap().rearrange("(r p) w -> p r w", p=128), [dkey("cc_dst%d" % l)], gath.k, "cc_g%d" % l)
            V(lambda e: e.tensor_scalar(out=cblk.ap, in0=gath.ap[:, 0, :], scalar1=selb.ap[:, 0:1], scalar2=None, op0=ALU.mult),
              gath.k + selb.k, cblk.k)
            for r in range(1, 8):
                V(lambda e, r=r: e.scalar_tensor_tensor(out=cblk.ap, in0=gath.ap[:, r, :], scalar=selb.ap[:, r:r + 1], in1=cblk.ap,
                                                        op0=ALU.mult, op1=ALU.add),
                  gath.k + selb.k + cblk.k, cblk.k)
        else:
            dma("sp", cblk.ap, carry_in[l], cblk.k, cblk.k, "carry_i")
        V(lambda e: e.tensor_copy(out=sin_b.ap, in_=cblk.ap[:, 0:32].rearrange("p (a b) -> p a b", a=2)), cblk.k, sin_b.k)
        V(lambda e: e.tensor_copy(out=upool0.ap[:, :, 0:16], in_=cblk.ap[:, 32:96].rearrange("p (g t) -> p g t", g=4)),
          cblk.k, upool0.k)
        level3(sin_b.ap[:, 0], sin_b.ap[:, 1], sin_b.k)
        propagate()
        if l == 0:
            dbg("Xe2", Xe)
            dbg("Fb", Fb)
        pool_tile(l, 0, upool0, 2)
        for t4 in range(4):
            ssm_group([t4 * 4, t4 * 4 + 1], True)
            ssm_group([t4 * 4 + 2, t4 * 4 + 3], True)
            gelu_tile(t4)
        for blk in range(4):
            for n in range(4):
                bank = n
                for kt in range(4):
                    T(lambda e, n=n, kt=kt, blk=blk, bank=bank: e.matmul(
                        ps[:, bank, :], lhsT=w_glu_b.ap[:, kt, n * 128:(n + 1) * 128],
                        rhs=ussm.ap[:, kt, 4 * blk:4 * blk + 4, :].rearrange("p a b -> p (a b)"),
                        start=(kt == 0), stop=(kt == 3)),
                      [w_glu_b.k[0], ussm.k[kt * 4 + blk]], [psk[bank]])
            for n in range(4):
                s = n % 2
                A(lambda e, n=n, s=s: e.activation(out=gt.ap[:, 0, s, :], in_=ps[:, n, :], func=AF.Sigmoid, bias=bglu.ap[:, l, n:n + 1]),
                  [psk[n], bglu.k[0]], [gt.k[s]])
                yv = ussm.ap[:, n, 4 * blk:4 * blk + 4, :].rearrange("p a b -> p (a b)")
                V(lambda e, yv=yv, s=s: e.tensor_mul(out=yv, in0=yv, in1=gt.ap[:, 0, s, :]),
                  [gt.k[s], ussm.k[n * 4 + blk]], [ussm.k[n * 4 + blk]])
        for blk in range(4):
            for n in range(DT):
                bank = (blk * DT + n) % 8
                for kt in range(DT):
                    src = ymp if kt < 4 else ussm
                    T(lambda e, n=n, kt=kt, blk=blk, bank=bank, src=src: e.matmul(
                        ps[:, bank, :], lhsT=w_out_b.ap[:, kt, n * 128:(n + 1) * 128],
                        rhs=src.ap[:, kt % 4, 4 * blk:4 * blk + 4, :].rearrange("p a b -> p (a b)"),
                        start=(kt == 0), stop=(kt == DT - 1)),
                      [w_out_b.k[kt], src.k[(kt % 4) * 4 + blk]], [psk[bank]])
                hv = h.ap[:, n, :].rearrange("p (c j) -> p j c", j=T1)[:, 4 * blk:4 * blk + 4, :]
                V(lambda e, hv=hv, bank=bank: e.tensor_add(out=hv, in0=hv, in1=ps[:, bank, :].rearrange("p (a b) -> p a b", a=4)),
                  [psk[bank]] + [hk(n, t) for t in range(TT)], [hk(n, t) for t in range(TT)])
        for sub in range(2):
            for tl in range(2):
                tt = sub * 2 + tl
                norm_tile(tt, g2.ap[:, l, :], sq2, rstd2.ap, rstd2.k, lambda dt, tl=tl: xn2.ap[:, dt, tl * 512:(tl + 1) * 512],
                          [xn2.k[dt * 2 + tl] for dt in range(DT)], 7)
            for f in range(FT):
                slot = f % 2
                for gi, wsrc in enumerate((w_gate, w_up)):
                    dma("pool", wgu.ap[:, slot, gi], wsrc[l, :, f * 128:(f + 1) * 128].rearrange("(kt p) n -> p kt n", p=128),
                        [], [wgu.k[slot]], "wgu%d" % slot)
                for tl in range(2):
                    bg = (f % 2) * 4 + tl * 2
                    for gi in range(2):
                        for kt in range(DT):
                            T(lambda e, gi=gi, kt=kt, tl=tl, slot=slot, bg=bg: e.matmul(
                                ps[:, bg + gi, :], lhsT=wgu.ap[:, slot, gi, kt, :], rhs=xn2.ap[:, kt, tl * 512:(tl + 1) * 512],
                                start=(kt == 0), stop=(kt == DT - 1)),
                              [wgu.k[slot], xn2.k[kt * 2 + tl]], [psk[bg + gi]])
                    s = tl
                    A(lambda e, bg=bg, s=s: e.activation(out=sg.ap[:, s, :], in_=ps[:, bg, :], func=AF.Silu), [psk[bg]], [sg.k[s]])
                    V(lambda e, bg=bg, s=s, f=f, tl=tl: e.tensor_mul(out=act.ap[:, f, tl * 512:(tl + 1) * 512], in0=sg.ap[:, s, :],
                                                                     in1=ps[:, bg + 1, :]),
                      [sg.k[s], psk[bg + 1]], [act.k[f * 2 + tl]])
            for n in range(DT):
                slot = n % 2
                dma("pool", wd.ap[:, slot], w_down[l, :, n * 128:(n + 1) * 128].rearrange("(kt p) n -> p kt n", p=128),
                    [], [wd.k[slot]], "wd%d" % slot)
                for tl in range(2):
                    tt = sub * 2 + tl
                    bank = (n * 2 + tl) % 8
                    for f in range(FT):
                        T(lambda e, f=f, tl=tl, slot=slot, bank=bank: e.matmul(
                            ps[:, bank, :], lhsT=wd.ap[:, slot, f, :], rhs=act.ap[:, f, tl * 512:(tl + 1) * 512],
                            start=(f == 0), stop=(f == FT - 1)),
                          [wd.k[slot], act.k[f * 2 + tl]], [psk[bank]])
                    hv = h.ap[:, n, tt * 512:(tt + 1) * 512]
                    V(lambda e, hv=hv, bank=bank: e.tensor_add(out=hv, in0=hv, in1=ps[:, bank, :]),
                      [psk[bank], hk(n, tt)], [hk(n, tt)])

    for l in range(nl):
        layer(l, l == nl - 1)

    finals = []
    for dt in range(DT):
        finals.append(dma("sp", hT_out[dt * 128:(dt + 1) * 128, :], h.ap[:, dt, :], [hk(dt, t) for t in range(TT)],
                          [dkey("hT_out")], "out_h"))
    for tt in range(TT):
        for dt in range(DT):
            s = dt % 2
            A(lambda e, dt=dt, s=s, tt=tt: e.activation(out=sq2.ap[:, s, :], in_=h.ap[:, dt, tt * 512:(tt + 1) * 512], func=AF.Square),
              [hk(dt, tt)], [sq2.k[s]])
            T(lambda e, dt=dt, s=s: e.matmul(ps[:, 7, :], lhsT=ones_bf.ap, rhs=sq2.ap[:, s, :], start=(dt == 0), stop=(dt == DT - 1)),
              [sq2.k[s], ones_bf.k[0]], [psk[7]])
        V(lambda e: e.tensor_scalar(out=rstd2.ap, in0=ps[:, 7, :], scalar1=1.0 / D, scalar2=EPS, op0=ALU.mult, op1=ALU.add),
          [psk[7]], rstd2.k)
        A(lambda e: e.activation(out=rstd2.ap, in_=rstd2.ap, func=AF.Sqrt), rstd2.k, rstd2.k)
        V(lambda e: e.reciprocal(out=rstd2.ap, in_=rstd2.ap), rstd2.k, rstd2.k)
        for dt in range(DT):
            s = dt % 2
            V(lambda e, dt=dt, s=s, tt=tt: e.scalar_tensor_tensor(out=sg.ap[:, s, :], in0=h.ap[:, dt, tt * 512:(tt + 1) * 512],
                                                                  scalar=gf.ap[:, dt:dt + 1], in1=rstd2.ap, op0=ALU.mult, op1=ALU.mult),
              [hk(dt, tt), rstd2.k[0], gf.k[0]], [sg.k[s]])
            finals.append(dma("sp", yT[dt * 128:(dt + 1) * 128, tt * 512:(tt + 1) * 512], sg.ap[:, s, :], [sg.k[s]],
                              [dkey("yT")], "out_y%d" % s))
    fw = {}
    for d in finals + dbg_finals + [dk["carry_out"].last_w]:
        fw[d[1]] = max(fw.get(d[1], 0), d[2])
    P.emit([("dma", n, v) for n, v in fw.items()])
    nc._names = P.names
    return nc


_CACHE = {}


def _consts():
    c = np.zeros((128, 512), np.float32)
    c[:, 0:128] = np.eye(128, dtype=np.float32)
    rows = np.arange(128)
    gi = (rows // 16) % 2
    c[:, 128] = (gi == 0)
    c[:, 129] = (gi == 1)
    c[:, 130:258] = 1.0
    return c


def _cnt_tab(first_half):
    t = np.zeros((128, 4, 16), np.float32)
    for g, w in enumerate((2, 4, 8, 16)):
        for i in range(16):
            t[:, g, i] = 1.0 / (min(i + 1, w) if first_half else w)
    return t.reshape(128, 64)


WNAMES = ["norm_mix", "w_in", "w_pool", "pool_scale", "lam_re", "lam_im", "log_dt", "b_re", "b_im", "c_re", "c_im",
          "d_skip", "w_glu", "b_glu", "w_out", "norm_ffn", "w_gate", "w_up", "w_down"]


MODE = "fused"


def _common_maps(x, nf):
    consts = _consts()
    maps = []
    for r in range(8):
        b, half = r // 2, r % 2
        m = {}
        m["xT"] = np.ascontiguousarray(x[b, half * NTOK:(half + 1) * NTOK, :].T)
        m["norm_final"] = nf
        m["consts"] = consts
        m["cnt_tab"] = _cnt_tab(half == 0)
        s = np.zeros((128, 8), np.float32)
        if half == 1:
            s[:, r - 1] = 1.0
        m["sel"] = s
        maps.append(m)
    return maps


def kernel(**inputs):
    x = np.asarray(inputs["x"], np.float32)
    W = {k: np.ascontiguousarray(np.asarray(inputs[k], np.float32)) for k in WNAMES}
    nf = np.ascontiguousarray(np.asarray(inputs["norm_final"], np.float32))
    out = np.empty((4, 4096, D), np.float32)
    base = _common_maps(x, nf)
    if MODE == "fused":
        key = ("fused",)
        if key not in _CACHE:
            _CACHE[key] = build(DEPTH, True)
        nc = _CACHE[key]
        in_maps = []
        for r in range(8):
            m = dict(base[r])
            m.update(W)
            m["carry_in"] = np.zeros((DEPTH, 128, CARRY_W), np.float32)
            in_maps.append(m)
        res = run_bass_kernel_spmd(nc, in_maps, core_ids=list(range(8)))
        for r in range(8):
            b, half = r // 2, r % 2
            out[b, half * NTOK:(half + 1) * NTOK, :] = np.asarray(res.results[r]["yT"]).T
        return out
    key = ("layer",)
    if key not in _CACHE:
        _CACHE[key] = build(1, False)
    nc = _CACHE[key]
    Wl = [{k: np.ascontiguousarray(W[k][l:l + 1]) for k in WNAMES} for l in range(DEPTH)]
    hT = [base[r]["xT"] for r in range(8)]
    carry = [np.zeros((1, 128, CARRY_W), np.float32) for _ in range(8)]
    for step in range(DEPTH + 1):
        in_maps = []
        lay = []
        for r in range(8):
            half = r % 2
            l = step if half == 0 else step - 1
            lay.append(l)
            lw = min(max(l, 0), DEPTH - 1)
            m = dict(base[r])
            m.update(Wl[lw])
            m["xT"] = hT[r]
            m["carry_in"] = carry[r]
            in_maps.append(m)
        res = run_bass_kernel_spmd(nc, in_maps, core_ids=list(range(8)))
        for r in range(8):
            l = lay[r]
            if l < 0 or l >= DEPTH:
                continue
            hT[r] = np.ascontiguousarray(np.asarray(res.results[r]["hT_out"], np.float32))
            if r % 2 == 0:
                carry[r + 1] = np.ascontiguousarray(np.asarray(res.results[r]["carry_out"], np.float32))
            if l == DEPTH - 1:
                b, half = r // 2, r % 2
                out[b, half * NTOK:(half + 1) * NTOK, :] = np.asarray(res.results[r]["yT"]).T
    return out
```

```python
import math
import numpy as np
import concourse.bass as bass
import concourse.mybir as mybir
from concourse.bass_utils import run_bass_kernel_spmd

F32 = mybir.dt.float32
BF16 = mybir.dt.bfloat16
ALU = mybir.AluOpType
AF = mybir.ActivationFunctionType

DEPTH = 4
D = 1024
DT = 8
NTOK = 2048
TT = 4
DFF = 2816
FT = 22
T1 = 16
NC1 = 128
NPAIR = 16
EPS = 1e-6
SBUF_BASE = 16512
SBUF_LIMIT = 229376
CARRY_W = 2 * NPAIR + 4 * 16
TWO_PI = 2.0 * math.pi


class Key:
    __slots__ = ("space", "lo", "hi", "name", "last_w", "readers", "ovl")

    def __init__(self, space, lo, hi, name):
        self.space, self.lo, self.hi, self.name = space, lo, hi, name
        self.last_w = None
        self.readers = []
        self.ovl = None


class Prog:
    ENG = ("pe", "act", "dve", "pool", "sp")

    def __init__(self, nc):
        self.nc = nc
        self.ops = {e: [] for e in self.ENG}
        self.keys = []
        self.dma_cnt = {}
        self.dma_sems = []
        self.names = {}

    def key(self, space, lo, hi, name):
        k = Key(space, lo, hi, name)
        self.keys.append(k)
        return k

    def _ovl(self, k):
        if k.ovl is None or k.ovl[0] != len(self.keys):
            lst = [o for o in self.keys if o.space == k.space and o.lo < k.hi and k.lo < o.hi]
            k.ovl = (len(self.keys), lst)
        return k.ovl[1]

    def op(self, eng, fn, reads=(), writes=(), dma=None, inc=16):
        deps = set()
        for k in reads:
            for o in self._ovl(k):
                if o.last_w is not None:
                    deps.add(o.last_w)
        for k in writes:
            for o in self._ovl(k):
                if o.last_w is not None:
                    deps.add(o.last_w)
                for r in o.readers:
                    deps.add(r)
        deps = set(("dma", d[1], self.dma_cnt[d[1]]) if d[0] == "dma" else d for d in deps)
        idx = len(self.ops[eng])
        if dma is not None:
            self.dma_cnt[dma] = self.dma_cnt.get(dma, 0) + inc
            me = ("dma", dma, self.dma_cnt[dma])
        else:
            me = ("eng", eng, idx)
        import sys as _sys
        fr = _sys._getframe(1)
        while fr.f_code.co_name in ("V", "A", "G", "T", "dma", "cmul", "cplx_axpy_step"):
            fr = fr.f_back
        rec = dict(fn=fn, deps=deps, dma=dma, me=me, marked=False, inc=inc, line=fr.f_lineno)
        self.ops[eng].append(rec)
        for k in writes:
            for o in self._ovl(k):
                if o is not k:
                    o.readers = []
            k.last_w = me
            k.readers = []
        for k in reads:
            if me[0] == "eng":
                k.readers = [r for r in k.readers if not (r[0] == "eng" and r[1] == me[1])]
            k.readers.append(me)
        return me

    def emit(self, final_waits):
        nc = self.nc
        for e in self.ENG:
            for rec in self.ops[e]:
                for d in rec["deps"]:
                    if d[0] == "eng":
                        if d[1] == "pe" and e == "pe":
                            continue
                        self.ops[d[1]][d[2]]["marked"] = True
        for d in final_waits:
            if d[0] == "eng":
                self.ops[d[1]][d[2]]["marked"] = True
        cnt_at = {}
        for e in self.ENG:
            c = 0
            for i, rec in enumerate(self.ops[e]):
                if rec["marked"] and rec["dma"] is None:
                    c += 1
                cnt_at[(e, i)] = c
        import contextlib
        with contextlib.ExitStack() as st:
            esem = {e: st.enter_context(nc.semaphore("s_" + e)) for e in self.ENG}
            dsem = {n: st.enter_context(nc.semaphore("d_" + n)) for n in self.dma_cnt}
            block = st.enter_context(nc.Block())

            def run(e, eng):
                seen = {}
                for i, rec in enumerate(self.ops[e]):
                    need = {}
                    for d in rec["deps"]:
                        if d[0] == "eng":
                            if d[1] == "pe" and e == "pe":
                                continue
                            if d[1] == e and d[2] >= i:
                                continue
                            s, v = esem[d[1]], cnt_at[(d[1], d[2])]
                        else:
                            s, v = dsem[d[1]], d[2]
                        if v > need.get(s, (0, None))[0]:
                            need[s] = (v, s)
                    for s, (v, _) in need.items():
                        if seen.get(s, 0) < v:
                            eng.wait_ge(s, v)
                            seen[s] = v
                    ins = rec["fn"](eng)
                    try:
                        self.names[ins.ins.name] = rec["line"]
                    except Exception:
                        pass
                    if rec["dma"] is not None:
                        ins.then_inc(dsem[rec["dma"]], rec["inc"])
                    elif rec["marked"]:
                        ins.then_inc(esem[e], 1)
                if e == "sp":
                    for d in final_waits:
                        if d[0] == "eng":
                            eng.wait_ge(esem[d[1]], cnt_at[(d[1], d[2])])
                        else:
                            eng.wait_ge(dsem[d[1]], d[2])

            @block.tensor
            def _(eng):
                run("pe", eng)

            @block.scalar
            def _(eng):
                run("act", eng)

            @block.vector
            def _(eng):
                run("dve", eng)

            @block.gpsimd
            def _(eng):
                run("pool", eng)

            @block.sync
            def _(eng):
                run("sp", eng)


class Buf:
    def __init__(self, P, name, shape, dtype, off, nkeys=1):
        self.P = P
        esz = 2 if dtype == BF16 else 4
        n = 1
        for s in shape:
            n *= s
        self.nbytes = n * esz
        self.off = off
        self.t = P.nc.alloc_sbuf_tensor_at(name, [128] + list(shape), dtype, offset=SBUF_BASE + off)
        self.ap = self.t.ap()
        assert SBUF_BASE + off + self.nbytes <= SBUF_LIMIT, name
        assert self.nbytes % nkeys == 0
        step = self.nbytes // nkeys
        self.k = [P.key("sb", off + i * step, off + (i + 1) * step, "%s.%d" % (name, i)) for i in range(nkeys)]

    @property
    def end(self):
        return self.off + self.nbytes


def build(nl, use_ag, debug=False):
    nc = bass.Bass("TRN2", target_bir_lowering=False)
    P = Prog(nc)

    def din(name, shape):
        return nc.dram_tensor(name, list(shape), F32, kind="ExternalInput").ap()

    xT = din("xT", [D, NTOK])
    norm_mix = din("norm_mix", [nl, D])
    w_in = din("w_in", [nl, D, D])
    w_pool = din("w_pool", [nl, 4, 128, 128])
    pool_scale = din("pool_scale", [nl, 512])
    lam_re = din("lam_re", [nl, 32, 64])
    lam_im = din("lam_im", [nl, 32, 64])
    log_dt = din("log_dt", [nl, 32])
    b_re = din("b_re", [nl, 32, 64, 16])
    b_im = din("b_im", [nl, 32, 64, 16])
    c_re = din("c_re", [nl, 32, 16, 64])
    c_im = din("c_im", [nl, 32, 16, 64])
    d_skip = din("d_skip", [nl, 512])
    w_glu = din("w_glu", [nl, 512, 512])
    b_glu = din("b_glu", [nl, 512])
    w_out = din("w_out", [nl, D, D])
    norm_ffn = din("norm_ffn", [nl, D])
    w_gate = din("w_gate", [nl, D, DFF])
    w_up = din("w_up", [nl, D, DFF])
    w_down = din("w_down", [nl, DFF, D])
    norm_final = din("norm_final", [D])
    consts = din("consts", [128, 512])
    cnt_tab = din("cnt_tab", [128, 64])
    sel = din("sel", [128, 8])
    carry_in = din("carry_in", [nl, 128, CARRY_W])
    yT = nc.dram_tensor("yT", [D, NTOK], F32, kind="ExternalOutput").ap()
    hT_out = nc.dram_tensor("hT_out", [D, NTOK], F32, kind="ExternalOutput").ap()
    carry_out = nc.dram_tensor("carry_out", [nl, 128, CARRY_W], F32, kind="ExternalOutput").ap()
    if use_ag:
        cc_src = [nc.dram_tensor("cc_src%d" % l, [128, CARRY_W], F32) for l in range(nl)]
        cc_dst = [nc.dram_tensor("cc_dst%d" % l, [8 * 128, CARRY_W], F32) for l in range(nl)]
    dk = {}

    def dkey(name):
        if name not in dk:
            dk[name] = P.key("dram", len(dk), len(dk) + 1, name)
        return dk[name]

    dbg_finals = []

    def dbg(name, buf):
        if not debug:
            return
        shp = list(buf.ap.shape)
        t = nc.dram_tensor("dbg_" + name, shp, buf.ap.dtype, kind="ExternalOutput").ap()
        dbg_finals.append(P.op("sp", lambda e: e.dma_start(out=t, in_=buf.ap), buf.k, [dkey("dbg_" + name)], dma="dbg_" + name))

    off = [0]

    def alloc(name, shape, dtype, nkeys=1, at=None):
        o = off[0] if at is None else at
        b = Buf(P, name, shape, dtype, o, nkeys)
        if at is None:
            off[0] = (b.end + 31) // 32 * 32
        return b

    h = alloc("h", [DT, NTOK], F32, nkeys=DT * TT)
    cst = alloc("cst", [512], F32)
    cntb = alloc("cntb", [4, 16], F32)
    selb = alloc("selb", [8], F32)
    g1 = alloc("g1", [nl, DT], F32)
    g2 = alloc("g2", [nl, DT], F32)
    gf = alloc("gf", [DT], F32)
    pscale = alloc("pscale", [nl, 4], F32)
    bglu = alloc("bglu", [nl, 4], F32)
    dskp = alloc("dskp", [nl, 4], F32)
    ident_bf = alloc("ident_bf", [128], BF16)
    ones_bf = alloc("ones_bf", [128], BF16)
    zeros = alloc("zeros", [2, NC1], F32)
    lamn = alloc("lamn", [3, 128], F32)
    prm = alloc("prm", [24, NPAIR], F32)
    prmi = alloc("prmi", [NPAIR], mybir.dt.int32)
    pw1 = alloc("pw1", [2, NPAIR, 16], F32)
    pw2 = alloc("pw2", [2, NPAIR, 16], F32)
    pwt = alloc("pwt", [4, NPAIR, 8], F32)
    WB = alloc("WB", [4, 2, 128], BF16)
    WC = alloc("WC", [NPAIR, 2, 32], BF16)
    diagD = alloc("diagD", [4, 128], BF16)
    Xe = alloc("Xe", [2, NPAIR, 1 + NC1], F32, nkeys=2 * NPAIR)
    Fb = alloc("Fb", [2, NPAIR, 9], F32)
    l3t = alloc("l3t", [4, NPAIR], F32)
    sin_b = alloc("sin_b", [2, NPAIR], F32)
    cblk = alloc("cblk", [CARRY_W], F32)
    gath = alloc("gath", [8, CARRY_W], F32)
    persist_end = off[0]
    w_in_b = alloc("w_in_b", [DT, D], BF16, nkeys=DT)
    w_glu_b = alloc("w_glu_b", [4, 512], BF16)
    w_pool_b = alloc("w_pool_b", [4, 128], BF16)
    common_end = off[0]
    regA = off[0]
    sq = alloc("sq", [2, 512], BF16, nkeys=2)
    rstd = alloc("rstd", [2, 512], F32, nkeys=2)
    xn1 = alloc("xn1", [2, DT, 512], BF16, nkeys=2 * DT)
    regA_end = off[0]
    off[0] = regA
    sring = alloc("sring", [4, 2, 2, NC1], F32, nkeys=8)
    stmp = alloc("stmp", [4, 2, NC1], F32, nkeys=4)
    l2t = alloc("l2t", [4, NPAIR * 16], F32)
    off[0] = max(off[0], regA_end)
    poolreg = off[0]
    upool = alloc("upool", [4, 16 + 512], F32, nkeys=4)
    upool0 = alloc("upool0", [4, 16 + 512], F32, nkeys=4)
    ps_s = alloc("ps_s", [2, 528], F32, nkeys=2)
    dpool = alloc("dpool", [4, 512], BF16, nkeys=4)
    poolreg_end = off[0]
    off[0] = poolreg
    bnat = alloc("bnat", [2, NPAIR, 32], F32)
    bbar = alloc("bbar", [2, NPAIR, 32], F32)
    btmp = alloc("btmp", [2, NPAIR, 32], F32)
    cnat = alloc("cnat", [4, 2, 64], F32)
    cin = alloc("cin", [4, 2, 128], F32)
    assert off[0] <= poolreg_end
    off[0] = poolreg
    sbf = alloc("sbf", [4, 2, 2, 2, NC1], BF16, nkeys=8)
    gt = alloc("gt", [2, 2, 512], F32, nkeys=4)
    assert off[0] <= poolreg_end
    off[0] = poolreg_end
    ussm = alloc("ussm", [4, T1, NC1], BF16, nkeys=16)
    ymp = alloc("ymp", [4, T1, NC1], BF16, nkeys=16)
    w_out_b = alloc("w_out_b", [DT, D], BF16, nkeys=DT, at=w_in_b.off)
    mixer_end = off[0]
    off[0] = w_glu_b.off
    sq2 = alloc("sq2", [2, 512], BF16, nkeys=2)
    rstd2 = alloc("rstd2", [512], F32)
    xn2 = alloc("xn2", [DT, 1024], BF16, nkeys=DT * 2)
    act = alloc("act", [FT, 1024], BF16, nkeys=FT * 2)
    sg = alloc("sg", [2, 512], F32, nkeys=2)
    wgu = alloc("wgu", [2, 2, DT, 128], BF16, nkeys=2)
    wd = alloc("wd", [2, FT, 128], BF16, nkeys=2)
    ffn_end = off[0]
    assert SBUF_BASE + max(mixer_end, ffn_end) <= SBUF_LIMIT, (mixer_end, ffn_end)

    ps_t = nc.alloc_psum_tensor("ps", [128, 8, 512], F32)
    ps = ps_t.ap()
    psk = [P.key("ps", b, b + 1, "ps%d" % b) for b in range(8)]

    ident = cst.ap[:, 0:128]
    m_gi = [cst.ap[:, 128:129], cst.ap[:, 129:130]]

    def V(fn, r, w):
        return P.op("dve", fn, r, w)

    def A(fn, r, w):
        return P.op("act", fn, r, w)

    def G(fn, r, w):
        return P.op("pool", fn, r, w)

    def T(fn, r, w):
        return P.op("pe", fn, r, w)

    def dma(q, out, in_, r, w, sem, **kw):
        return P.op(q, lambda e: e.dma_start(out=out, in_=in_, **kw), r, w, dma=sem)

    def hk(dt, tt):
        return h.k[dt * TT + tt]

    def bc(ap2, shape):
        return ap2.broadcast_to(shape)

    dma("sp", cst.ap, consts, [], cst.k, "cst")
    dma("sp", cntb.ap, cnt_tab.rearrange("p (g t) -> p g t", g=4), [], cntb.k, "cst")
    dma("sp", selb.ap, sel, [], selb.k, "cst")
    for dt in range(DT):
        dma("sp", h.ap[:, dt, :], xT[dt * 128:(dt + 1) * 128, :], [], [hk(dt, t) for t in range(TT)], "hload")

    def small_T(dst, src, n):
        dma("act", dst, src.rearrange("l (t c) -> c l t", c=128), [], [], "cst",
            allow_slow_non_contiguous=True)

    small_T(g1.ap, norm_mix, DT)
    small_T(g2.ap, norm_ffn, DT)
    small_T(pscale.ap, pool_scale, 4)
    small_T(bglu.ap, b_glu, 4)
    small_T(dskp.ap, d_skip, 4)
    P.op("act", lambda e: e.dma_start(out=gf.ap, in_=norm_final.rearrange("(t c) -> c t", c=128),
                                      allow_slow_non_contiguous=True),
         [], g1.k + g2.k + gf.k + pscale.k + bglu.k + dskp.k, dma="cst")
    V(lambda e: e.tensor_copy(out=ident_bf.ap, in_=ident), cst.k, ident_bf.k)
    V(lambda e: e.memset(ones_bf.ap, 1.0), [], ones_bf.k)
    V(lambda e: e.memset(zeros.ap, 0.0), [], zeros.k)

    def norm_tile(tt, gain_ap, sqb, rstd_ap, rstd_k, xn_of, xn_keys, bank):
        for dt in range(DT):
            s = dt % 2
            A(lambda e, dt=dt, s=s: e.activation(out=sqb.ap[:, s, :], in_=h.ap[:, dt, tt * 512:(tt + 1) * 512],
                                                 func=AF.Square),
              [hk(dt, tt)], [sqb.k[s]])
            T(lambda e, dt=dt, s=s: e.matmul(ps[:, bank, :], lhsT=ones_bf.ap, rhs=sqb.ap[:, s, :],
                                             start=(dt == 0), stop=(dt == DT - 1)),
              [sqb.k[s], ones_bf.k[0]], [psk[bank]])
        V(lambda e: e.tensor_scalar(out=rstd_ap, in0=ps[:, bank, :], scalar1=1.0 / D, scalar2=EPS,
                                    op0=ALU.mult, op1=ALU.add),
          [psk[bank]], rstd_k)
        A(lambda e: e.activation(out=rstd_ap, in_=rstd_ap, func=AF.Sqrt), rstd_k, rstd_k)
        V(lambda e: e.reciprocal(out=rstd_ap, in_=rstd_ap), rstd_k, rstd_k)
        for dt in range(DT):
            V(lambda e, dt=dt: e.scalar_tensor_tensor(out=xn_of(dt), in0=h.ap[:, dt, tt * 512:(tt + 1) * 512],
                                                      scalar=gain_ap[:, dt:dt + 1], in1=rstd_ap,
                                                      op0=ALU.mult, op1=ALU.mult),
              [hk(dt, tt), g1.k[0]] + rstd_k, [xn_keys[dt]])

    POOLW = (2, 4, 8, 16)

    def pool_tile(l, tt, ub, bank0):
        for g in range(4):
            w = POOLW[g]
            u = ub.ap[:, g, :]
            cur, curk = u, ub.k[g]
            sh = 1
            si = 0
            while sh < w:
                dst = ps_s.ap[:, si, :]
                V(lambda e, cur=cur, dst=dst, sh=sh: e.tensor_add(out=dst[:, sh:528], in0=cur[:, sh:528], in1=cur[:, 0:528 - sh]),
                  [curk], [ps_s.k[si]])
                cur, curk = dst, ps_s.k[si]
                si ^= 1
                sh *= 2
            V(lambda e, cur=cur, u=u, g=g, w=w: e.scalar_tensor_tensor(out=dpool.ap[:, g, :], in0=cur[:, 16:528], scalar=1.0 / w,
                                                                        in1=u[:, 16:528], op0=ALU.mult, op1=ALU.subtract),
              [curk, ub.k[g]], [dpool.k[g]])
            if tt == 0:
                dst = ps_s.ap[:, si, :]
                V(lambda e, cur=cur, dst=dst, g=g: e.tensor_mul(out=dst[:, 0:16], in0=cur[:, 16:32], in1=cntb.ap[:, g, :]),
                  [curk, cntb.k[0]], [ps_s.k[si]])
                V(lambda e, dst=dst, u=u, g=g: e.tensor_sub(out=dpool.ap[:, g, 0:16], in0=dst[:, 0:16], in1=u[:, 16:32]),
                  [ps_s.k[si], ub.k[g]], [dpool.k[g]])
            bank = bank0 + (g % 2)
            T(lambda e, g=g, bank=bank: e.matmul(ps[:, bank, :], lhsT=w_pool_b.ap[:, g, :], rhs=dpool.ap[:, g, :],
                                                 start=True, stop=True),
              [dpool.k[g], w_pool_b.k[0]], [psk[bank]])
            A(lambda e, g=g, bank=bank: e.activation(out=ymp.ap[:, g, :, tt * 32:(tt + 1) * 32],
                                                     in_=ps[:, bank, :].rearrange("p (c j) -> p j c", j=T1),
                                                     func=AF.Copy, scale=pscale.ap[:, l, g:g + 1]),
              [psk[bank], pscale.k[0]], [ymp.k[g * 4 + b] for b in range(4)])

    def cmul(eng_op, out_r, out_i, a_r, a_i, b_r, b_i, t1, t2, rk, wk, tk):
        eng_op(lambda e: e.tensor_mul(out=t1, in0=a_r, in1=b_r), rk, tk)
        eng_op(lambda e: e.tensor_mul(out=t2, in0=a_i, in1=b_i), rk, tk)
        eng_op(lambda e: e.tensor_sub(out=out_r, in0=t1, in1=t2), tk, wk)
        eng_op(lambda e: e.tensor_mul(out=t1, in0=a_r, in1=b_i), rk, tk)
        eng_op(lambda e: e.tensor_mul(out=t2, in0=a_i, in1=b_r), rk, tk)
        eng_op(lambda e: e.tensor_add(out=out_i, in0=t1, in1=t2), tk, wk)

    def ssm_setup(l):
        S = lambda i: prm.ap[:, i, :]
        pk = prm.k
        for i, src in enumerate((lam_re, lam_im)):
            dma("sp", lamn.ap[0:16, i, :], src[l].rearrange("(pr gi) p -> pr (gi p)", gi=2), [], lamn.k, "lam")
        dma("sp", lamn.ap[0:1, 2, 0:32], log_dt[l].unsqueeze(0), [], lamn.k, "lam")
        T(lambda e: e.matmul(ps[:, 0, 32:64], lhsT=cst.ap[0:1, 130:258], rhs=lamn.ap[0:1, 2, 0:32], start=True, stop=True),
          lamn.k + cst.k, [psk[0]])
        for i in range(2):
            T(lambda e, i=i: e.transpose(ps[:, 0, i * 16:(i + 1) * 16], lamn.ap[0:16, i, :], ident[0:16, 0:16]),
              lamn.k + cst.k, [psk[0]])
        V(lambda e: e.tensor_copy(out=prm.ap[:, 0:2, :], in_=ps[:, 0, 0:32].rearrange("p (a b) -> p a b", a=2)),
          [psk[0]], pk)
        for gi in range(2):
            V(lambda e, gi=gi: e.tensor_copy(out=prm.ap[gi * 64:(gi + 1) * 64, 2, :],
                                             in_=ps[gi * 64:(gi + 1) * 64, 0, 32:64].rearrange("p (pr gi) -> p pr gi", gi=2)[:, :, gi]),
              [psk[0]], pk)
        lr, li, dtt = S(0), S(1), S(2)
        A(lambda e: e.activation(out=dtt, in_=dtt, func=AF.Exp), pk, pk)
        V(lambda e: e.tensor_mul(out=S(3), in0=lr, in1=dtt), pk, pk)
        A(lambda e: e.activation(out=S(3), in_=S(3), func=AF.Exp), pk, pk)
        V(lambda e: e.tensor_mul(out=S(4), in0=li, in1=dtt), pk, pk)
        V(lambda e: e.tensor_scalar(out=S(5), in0=S(4), scalar1=0.125, scalar2=3.1415, op0=ALU.mult, op1=ALU.min), pk, pk)
        V(lambda e: e.tensor_scalar(out=S(6), in0=S(5), scalar1=-1.0, scalar2=0.5 * math.pi, op0=ALU.mult, op1=ALU.add), pk, pk)
        A(lambda e: e.activation(out=S(5), in_=S(5), func=AF.Sin), pk, pk)
        A(lambda e: e.activation(out=S(6), in_=S(6), func=AF.Sin), pk, pk)
        for _ in range(3):
            V(lambda e: e.tensor_mul(out=S(10), in0=S(5), in1=S(6)), pk, pk)
            V(lambda e: e.tensor_mul(out=S(11), in0=S(5), in1=S(5)), pk, pk)
            V(lambda e: e.tensor_mul(out=S(6), in0=S(6), in1=S(6)), pk, pk)
            V(lambda e: e.tensor_sub(out=S(6), in0=S(6), in1=S(11)), pk, pk)
            V(lambda e: e.tensor_scalar(out=S(5), in0=S(10), scalar1=2.0, scalar2=None, op0=ALU.mult), pk, pk)
        ar, ai, nai = S(7), S(8), S(9)
        V(lambda e: e.tensor_mul(out=ar, in0=S(6), in1=S(3)), pk, pk)
        V(lambda e: e.tensor_mul(out=ai, in0=S(5), in1=S(3)), pk, pk)
        V(lambda e: e.scalar_tensor_tensor(out=nai, in0=S(5), scalar=-1.0, in1=S(3), op0=ALU.mult, op1=ALU.mult), pk, pk)
        V(lambda e: e.tensor_mul(out=S(10), in0=lr, in1=lr), pk, pk)
        V(lambda e: e.tensor_mul(out=S(11), in0=li, in1=li), pk, pk)
        V(lambda e: e.tensor_add(out=S(10), in0=S(10), in1=S(11)), pk, pk)
        V(lambda e: e.reciprocal(out=S(10), in_=S(10)), pk, pk)
        V(lambda e: e.tensor_scalar_add(out=S(11), in0=ar, scalar1=-1.0), pk, pk)
        V(lambda e: e.tensor_mul(out=S(12), in0=S(11), in1=lr), pk, pk)
        V(lambda e: e.tensor_mul(out=S(13), in0=ai, in1=li), pk, pk)
        V(lambda e: e.tensor_add(out=S(12), in0=S(12), in1=S(13)), pk, pk)
        V(lambda e: e.tensor_mul(out=S(12), in0=S(12), in1=S(10)), pk, pk)
        V(lambda e: e.tensor_mul(out=S(13), in0=ai, in1=lr), pk, pk)
        V(lambda e: e.tensor_mul(out=S(14), in0=S(11), in1=li), pk, pk)
        V(lambda e: e.tensor_sub(out=S(13), in0=S(13), in1=S(14)), pk, pk)
        V(lambda e: e.tensor_mul(out=S(13), in0=S(13), in1=S(10)), pk, pk)
        V(lambda e: e.memset(bnat.ap, 0.0), [], bnat.k)
        for part, src in enumerate((b_re, b_im)):
            for gi in range(2):
                dma("sp", bnat.ap[gi * 64:(gi + 1) * 64, part, :, gi * 16:(gi + 1) * 16],
                    src[l].rearrange("(pr gi) p h -> gi p pr h", gi=2)[gi],
                    [], bnat.k, "bload", allow_slow_non_contiguous=True)
        sh3 = [128, NPAIR, 32]
        cr = S(12).unsqueeze(2).broadcast_to(sh3)
        ci = S(13).unsqueeze(2).broadcast_to(sh3)
        cmul(V, bbar.ap[:, 0], bbar.ap[:, 1], cr, ci, bnat.ap[:, 0], bnat.ap[:, 1], btmp.ap[:, 0], btmp.ap[:, 1],
             pk + bnat.k, bbar.k, btmp.k)
        for t4 in range(4):
            for part in range(2):
                bank = 1 + (t4 * 2 + part) % 2
                T(lambda e, t4=t4, part=part, bank=bank: e.transpose(
                    ps[:, bank, 0:128], bbar.ap[:, part, t4 * 4:(t4 + 1) * 4, :].rearrange("p a b -> p (a b)"), ident),
                  bbar.k + cst.k, [psk[bank]])
                A(lambda e, t4=t4, part=part, bank=bank: e.copy(out=WB.ap[:, t4, part, :], in_=ps[:, bank, 0:128]),
                  [psk[bank]], WB.k)
        for part, src in enumerate((c_re, c_im)):
            dma("sp", cnat.ap[:, :, part, :], src[l].rearrange("(t r) h p -> (r h) t p", t=4), [], cnat.k, "cload")
        for gi in range(2):
            V(lambda e, gi=gi: e.tensor_scalar(out=cin.ap[:, :, :, gi * 64:(gi + 1) * 64], in0=cnat.ap,
                                               scalar1=m_gi[gi], scalar2=None, op0=ALU.mult),
              cnat.k + cst.k, cin.k)
        for t4 in range(4):
            for part in range(2):
                bank = 1 + (t4 * 2 + part) % 2
                T(lambda e, t4=t4, part=part, bank=bank: e.transpose(ps[:, bank, 0:128], cin.ap[:, t4, part, :], ident),
                  cin.k + cst.k, [psk[bank]])
                sc = 1.0 if part == 0 else -1.0
                A(lambda e, t4=t4, part=part, bank=bank, sc=sc: e.activation(
                    out=WC.ap[:, t4 * 4:(t4 + 1) * 4, part, :], in_=ps[:, bank, 0:128].rearrange("p (a b) -> p a b", a=4),
                    func=AF.Copy, scale=sc),
                  [psk[bank]], WC.k)
        for t4 in range(4):
            V(lambda e, t4=t4: e.tensor_scalar(out=diagD.ap[:, t4, :], in0=ident, scalar1=dskp.ap[:, l, t4:t4 + 1],
                                               scalar2=None, op0=ALU.mult),
              cst.k + dskp.k, diagD.k)
        def powers(pw, base_r, base_i, rk):
            V(lambda e: e.tensor_copy(out=pw.ap[:, 0, :, 0], in_=base_r), rk, pw.k)
            V(lambda e: e.tensor_copy(out=pw.ap[:, 1, :, 0], in_=base_i), rk, pw.k)
            n = 1
            while n < 16:
                shp = [128, NPAIR, n]
                br = pw.ap[:, 0, :, n - 1:n].broadcast_to(shp)
                bi = pw.ap[:, 1, :, n - 1:n].broadcast_to(shp)
                cmul(V, pw.ap[:, 0, :, n:2 * n], pw.ap[:, 1, :, n:2 * n], pw.ap[:, 0, :, 0:n], pw.ap[:, 1, :, 0:n], br, bi,
                     pwt.ap[:, 0, :, 0:n], pwt.ap[:, 1, :, 0:n], pw.k, pw.k, pwt.k)
                n *= 2
        powers(pw1, ar, ai, pk)
        powers(pw2, pw1.ap[:, 0, :, 15], pw1.ap[:, 1, :, 15], pw1.k)

    def ssm_group(prs, final):
        for blk in range(4):
            for L, pr in enumerate(prs):
                t4, q = pr // 4, pr % 4
                zb = 2 * L
                for part in range(2):
                    T(lambda e, part=part, blk=blk, zb=zb, t4=t4, q=q: e.matmul(
                        ps[:, zb + part, :], lhsT=WB.ap[32 * q:32 * q + 32, t4, part, :],
                        rhs=ussm.ap[32 * q:32 * q + 32, t4, 4 * blk:4 * blk + 4, :].rearrange("p a b -> p (a b)"),
                        start=True, stop=True, tile_position=(32 * q, 0)),
                      [ussm.k[t4 * 4 + blk], WB.k[0]], [psk[zb + part]])
            for jj in range(4):
                j = 4 * blk + jj
                ctx = []
                for L, pr in enumerate(prs):
                    zb = 2 * L
                    c = dict(pr=pr, zb=zb, bu=ps[:, zb:zb + 2, jj * 128:(jj + 1) * 128],
                             ar=prm.ap[:, 7, pr:pr + 1], ai=prm.ap[:, 8, pr:pr + 1], nai=prm.ap[:, 9, pr:pr + 1])
                    if j == 0:
                        if final:
                            c["prev"], c["prevk"] = Xe.ap[:, :, pr, 0:NC1], [Xe.k[pr], Xe.k[NPAIR + pr]]
                        else:
                            c["prev"], c["prevk"] = zeros.ap, zeros.k
                    else:
                        c["prev"], c["prevk"] = sring.ap[:, L, (j - 1) % 2], [sring.k[L * 2 + (j - 1) % 2]]
                    if (not final) and j == T1 - 1:
                        c["cur"], c["curk"] = Xe.ap[:, :, pr, 1:1 + NC1], [Xe.k[pr], Xe.k[NPAIR + pr]]
                    else:
                        c["cur"], c["curk"] = sring.ap[:, L, j % 2], [sring.k[L * 2 + j % 2]]
                    c["tm"], c["tmk"] = stmp.ap[:, L], [stmp.k[L]]
                    ctx.append(c)
                for c in ctx:
                    V(lambda e, c=c: e.scalar_tensor_tensor(out=c["tm"], in0=c["prev"], scalar=c["ar"], in1=c["bu"],
                                                            op0=ALU.mult, op1=ALU.add),
                      c["prevk"] + [psk[c["zb"]], psk[c["zb"] + 1], prm.k[0]], c["tmk"])
                for c in ctx:
                    V(lambda e, c=c: e.scalar_tensor_tensor(out=c["cur"][:, 0], in0=c["prev"][:, 1], scalar=c["nai"], in1=c["tm"][:, 0],
                                                            op0=ALU.mult, op1=ALU.add),
                      c["prevk"] + c["tmk"], c["curk"])
                for c in ctx:
                    V(lambda e, c=c: e.scalar_tensor_tensor(out=c["cur"][:, 1], in0=c["prev"][:, 0], scalar=c["ai"], in1=c["tm"][:, 1],
                                                            op0=ALU.mult, op1=ALU.add),
                      c["prevk"] + c["tmk"], c["curk"])
                if final:
                    for L, c in enumerate(ctx):
                        A(lambda e, c=c, j=j, L=L: e.copy(out=sbf.ap[:, L, :, j, :], in_=c["cur"]),
                          c["curk"], [sbf.k[(L * 2 + part) * 4 + blk] for part in range(2)])
            if final:
                yb = 4 + blk
                for L, pr in enumerate(prs):
                    t4, q = pr // 4, pr % 4
                    rhs_u = ussm.ap[32 * q:32 * q + 32, t4, 4 * blk:4 * blk + 4, :].rearrange("p a b -> p (a b)")
                    for part in range(2):
                        T(lambda e, part=part, blk=blk, yb=yb, pr=pr, q=q, L=L: e.matmul(
                            ps[32 * q:32 * q + 32, yb, :], lhsT=WC.ap[:, pr, part, :],
                            rhs=sbf.ap[:, L, part, 4 * blk:4 * blk + 4, :].rearrange("p a b -> p (a b)"),
                            start=(part == 0), stop=False, tile_position=(0, 32 * q)),
                          [sbf.k[(L * 2 + part) * 4 + blk], WC.k[0]], [psk[yb]])
                    T(lambda e, yb=yb, rhs_u=rhs_u, t4=t4, q=q: e.matmul(
                        ps[32 * q:32 * q + 32, yb, :], lhsT=diagD.ap[32 * q:32 * q + 32, t4, 32 * q:32 * q + 32], rhs=rhs_u,
                        start=False, stop=True, tile_position=(32 * q, 32 * q)),
                      [ussm.k[t4 * 4 + blk], diagD.k[0]], [psk[yb]])

    def ssm_final_tile(t4):
        prs = [t4 * 4 + q for q in range(4)]
        def zmm(blk):
            for L, pr in enumerate(prs):
                q = L
                for part in range(2):
                    T(lambda e, part=part, blk=blk, L=L, q=q: e.matmul(
                        ps[:, L, part * 256:(part + 1) * 256], lhsT=WB.ap[32 * q:32 * q + 32, t4, part, :],
                        rhs=ussm.ap[32 * q:32 * q + 32, t4, 2 * blk:2 * blk + 2, :].rearrange("p a b -> p (a b)"),
                        start=True, stop=True, tile_position=(32 * q, 0)),
                      [ussm.k[t4 * 4 + blk // 2], WB.k[0]], [psk[L]])

        zmm(0)
        for blk in range(8):
            bp = blk % 2
            for jj in range(2):
                j = 2 * blk + jj
                ctx = []
                for L, pr in enumerate(prs):
                    c = dict(pr=pr, L=L, bu=ps[:, L, :].rearrange("p (a b c) -> p a b c", a=2, b=2)[:, :, jj, :],
                             ar=prm.ap[:, 7, pr:pr + 1], ai=prm.ap[:, 8, pr:pr + 1], nai=prm.ap[:, 9, pr:pr + 1])
                    if j == 0:
                        c["prev"], c["prevk"] = Xe.ap[:, :, pr, 0:NC1], [Xe.k[pr], Xe.k[NPAIR + pr]]
                    else:
                        c["prev"], c["prevk"] = sring.ap[:, L, (j - 1) % 2], [sring.k[L * 2 + (j - 1) % 2]]
                    c["cur"], c["curk"] = sring.ap[:, L, j % 2], [sring.k[L * 2 + j % 2]]
                    c["tm"], c["tmk"] = stmp.ap[:, L], [stmp.k[L]]
                    ctx.append(c)
                for c in ctx:
                    V(lambda e, c=c: e.scalar_tensor_tensor(out=c["tm"], in0=c["prev"], scalar=c["ar"], in1=c["bu"],
                                                            op0=ALU.mult, op1=ALU.add),
                      c["prevk"] + [psk[c["L"]], prm.k[0]], c["tmk"])
                for c in ctx:
                    V(lambda e, c=c: e.scalar_tensor_tensor(out=c["cur"][:, 0], in0=c["prev"][:, 1], scalar=c["nai"], in1=c["tm"][:, 0],
                                                            op0=ALU.mult, op1=ALU.add),
                      c["prevk"] + c["tmk"], c["curk"])
                for c in ctx:
                    V(lambda e, c=c: e.scalar_tensor_tensor(out=c["cur"][:, 1], in0=c["prev"][:, 0], scalar=c["ai"], in1=c["tm"][:, 1],
                                                            op0=ALU.mult, op1=ALU.add),
                      c["prevk"] + c["tmk"], c["curk"])
                for c in ctx:
                    L = c["L"]
                    A(lambda e, c=c, jj=jj, L=L, bp=bp: e.copy(out=sbf.ap[:, L, bp, :, jj, :], in_=c["cur"]),
                      c["curk"], [sbf.k[L * 2 + bp]])
            if blk + 1 < 8:
                zmm(blk + 1)
            yb = 4 + blk // 2
            ycols = slice((blk % 2) * 256, (blk % 2 + 1) * 256)
            for L, pr in enumerate(prs):
                q = L
                rhs_u = ussm.ap[32 * q:32 * q + 32, t4, 2 * blk:2 * blk + 2, :].rearrange("p a b -> p (a b)")
                for part in range(2):
                    T(lambda e, part=part, yb=yb, ycols=ycols, pr=pr, q=q, L=L, bp=bp: e.matmul(
                        ps[32 * q:32 * q + 32, yb, ycols], lhsT=WC.ap[:, pr, part, :],
                        rhs=sbf.ap[:, L, bp, part, :, :].rearrange("p a b -> p (a b)"),
                        start=(part == 0), stop=False, tile_position=(0, 32 * q)),
                      [sbf.k[L * 2 + bp], WC.k[0]], [psk[yb]])
                T(lambda e, yb=yb, ycols=ycols, rhs_u=rhs_u, q=q: e.matmul(
                    ps[32 * q:32 * q + 32, yb, ycols], lhsT=diagD.ap[32 * q:32 * q + 32, t4, 32 * q:32 * q + 32], rhs=rhs_u,
                    start=False, stop=True, tile_position=(32 * q, 32 * q)),
                  [ussm.k[t4 * 4 + blk // 2], diagD.k[0]], [psk[yb]])

    def gelu_tile(t4):
        for blk in range(4):
            yb = 4 + blk
            y = ps[:, yb, :]
            sg_ = 0
            t_a = gt.ap[:, sg_, 0, :]
            t_b = gt.ap[:, sg_, 1, :]
            tk = [gt.k[sg_ * 2], gt.k[sg_ * 2 + 1]]
            A(lambda e, y=y, t_a=t_a: e.activation(out=t_a, in_=y, func=AF.Square), [psk[yb]], tk)
            V(lambda e, t_a=t_a: e.tensor_scalar(out=t_a, in0=t_a, scalar1=0.044715, scalar2=1.0, op0=ALU.mult, op1=ALU.add), tk, tk)
            V(lambda e, y=y, t_a=t_a: e.tensor_mul(out=t_a, in0=t_a, in1=y), tk + [psk[yb]], tk)
            A(lambda e, t_a=t_a, t_b=t_b: e.activation(out=t_b, in_=t_a, func=AF.Sigmoid, scale=2.0 * math.sqrt(2.0 / math.pi)), tk, tk)
            V(lambda e, y=y, t_b=t_b, blk=blk: e.tensor_mul(out=ussm.ap[:, t4, 4 * blk:4 * blk + 4, :].rearrange("p a b -> p (a b)"),
                                                            in0=t_b, in1=y),
              tk + [psk[yb]], [ussm.k[t4 * 4 + blk]])

    def cplx_axpy_step(cur_r, cur_i, prev_r, prev_i, m_r, m_i, shape, rk, wk):
        t = [l2t.ap[:, i, 0:shape[0] * shape[1]].rearrange("p (a b) -> p a b", a=shape[0]) for i in range(4)]
        tk = l2t.k
        V(lambda e: e.tensor_mul(out=t[0], in0=m_r, in1=prev_r), rk, tk)
        V(lambda e: e.tensor_mul(out=t[1], in0=m_i, in1=prev_i), rk, tk)
        V(lambda e: e.tensor_sub(out=t[0], in0=t[0], in1=t[1]), tk, tk)
        V(lambda e: e.tensor_add(out=cur_r, in0=cur_r, in1=t[0]), tk + wk, wk)
        V(lambda e: e.tensor_mul(out=t[2], in0=m_i, in1=prev_r), rk, tk)
        V(lambda e: e.tensor_mul(out=t[3], in0=m_r, in1=prev_i), rk, tk)
        V(lambda e: e.tensor_add(out=t[2], in0=t[2], in1=t[3]), tk, tk)
        V(lambda e: e.tensor_add(out=cur_i, in0=cur_i, in1=t[2]), tk + wk, wk)

    def level2():
        al_r = pw1.ap[:, 0, :, 15:16].broadcast_to([128, NPAIR, 8])
        al_i = pw1.ap[:, 1, :, 15:16].broadcast_to([128, NPAIR, 8])
        for j2 in range(1, 16):
            cr = Xe.ap[:, 0, :, 1 + j2:1 + NC1:16]
            ci = Xe.ap[:, 1, :, 1 + j2:1 + NC1:16]
            pr_ = Xe.ap[:, 0, :, j2:NC1:16]
            pi_ = Xe.ap[:, 1, :, j2:NC1:16]
            cplx_axpy_step(cr, ci, pr_, pi_, al_r, al_i, [NPAIR, 8], Xe.k + pw1.k, Xe.k)

    def level3(init_r, init_i, initk):
        V(lambda e: e.tensor_copy(out=Fb.ap[:, 0, :, 0], in_=init_r), initk, Fb.k)
        V(lambda e: e.tensor_copy(out=Fb.ap[:, 1, :, 0], in_=init_i), initk, Fb.k)
        a2r = pw2.ap[:, 0, :, 15]
        a2i = pw2.ap[:, 1, :, 15]
        t = [l3t.ap[:, i] for i in range(4)]
        tk = l3t.k
        for c2 in range(8):
            er = Xe.ap[:, 0, :, 16 * c2 + 16]
            ei = Xe.ap[:, 1, :, 16 * c2 + 16]
            fr, fi = Fb.ap[:, 0, :, c2], Fb.ap[:, 1, :, c2]
            nr, ni = Fb.ap[:, 0, :, c2 + 1], Fb.ap[:, 1, :, c2 + 1]
            rk = Fb.k + pw2.k + Xe.k
            V(lambda e, fr=fr: e.tensor_mul(out=t[0], in0=a2r, in1=fr), rk, tk)
            V(lambda e, fi=fi: e.tensor_mul(out=t[1], in0=a2i, in1=fi), rk, tk)
            V(lambda e: e.tensor_sub(out=t[0], in0=t[0], in1=t[1]), tk, tk)
            V(lambda e, nr=nr, er=er: e.tensor_add(out=nr, in0=t[0], in1=er), tk + rk, Fb.k)
            V(lambda e, fr=fr: e.tensor_mul(out=t[2], in0=a2i, in1=fr), rk, tk)
            V(lambda e, fi=fi: e.tensor_mul(out=t[3], in0=a2r, in1=fi), rk, tk)
            V(lambda e: e.tensor_add(out=t[2], in0=t[2], in1=t[3]), tk, tk)
            V(lambda e, ni=ni, ei=ei: e.tensor_add(out=ni, in0=t[2], in1=ei), tk + rk, Fb.k)

    def propagate():
        for c2 in range(8):
            cr = Xe.ap[:, 0, :, 1 + 16 * c2:17 + 16 * c2]
            ci = Xe.ap[:, 1, :, 1 + 16 * c2:17 + 16 * c2]
            fr = Fb.ap[:, 0, :, c2:c2 + 1].broadcast_to([128, NPAIR, 16])
            fi = Fb.ap[:, 1, :, c2:c2 + 1].broadcast_to([128, NPAIR, 16])
            cplx_axpy_step(cr, ci, fr, fi, pw2.ap[:, 0], pw2.ap[:, 1], [NPAIR, 16], Xe.k + pw2.k + Fb.k, Xe.k)
        V(lambda e: e.tensor_copy(out=Xe.ap[:, :, :, 0], in_=Fb.ap[:, :, :, 0]), Fb.k, Xe.k)

    def layer(l, last):
        for kt in range(DT):
            dma("pool", w_in_b.ap[:, kt, :], w_in[l, kt * 128:(kt + 1) * 128, :], [], [w_in_b.k[kt]], "w_in")
        dma("pool", w_pool_b.ap, w_pool[l].rearrange("g c d -> c g d"), [], w_pool_b.k, "w_pool")
        dma("pool", w_glu_b.ap, w_glu[l].rearrange("(kt p) n -> p kt n", p=128), [], w_glu_b.k, "w_glu")
        ssm_setup(l)
        if l == 0:
            for nm, b in (("prm", prm), ("pw1", pw1), ("pw2", pw2), ("WB", WB), ("WC", WC), ("diagD", diagD)):
                dbg(nm, b)
        def norm1(tt):
            sl = tt % 2
            norm_tile(tt, g1.ap[:, l, :], sq, rstd.ap[:, sl, :], [rstd.k[sl]], lambda dt, sl=sl: xn1.ap[:, sl, dt, :],
                      [xn1.k[sl * DT + dt] for dt in range(DT)], 7)

        norm1(0)
        for tt in range(TT):
            if tt + 1 < TT:
                norm1(tt + 1)
            sl = tt % 2
            ub = upool0 if tt == 0 else upool
            for n in (4, 5, 6, 7, 0, 1, 2, 3):
                bank = n % 4
                for kt in range(DT):
                    T(lambda e, n=n, kt=kt, bank=bank, sl=sl: e.matmul(ps[:, bank, :], lhsT=w_in_b.ap[:, kt, n * 128:(n + 1) * 128],
                                                                       rhs=xn1.ap[:, sl, kt, :], start=(kt == 0), stop=(kt == DT - 1)),
                      [w_in_b.k[kt], xn1.k[sl * DT + kt]], [psk[bank]])
                if n < 4:
                    A(lambda e, n=n, bank=bank, ub=ub: e.copy(out=ub.ap[:, n, 16:528], in_=ps[:, bank, :]), [psk[bank]], [ub.k[n]])
                else:
                    A(lambda e, n=n, bank=bank, tt=tt: e.copy(out=ussm.ap[:, n - 4, :, tt * 32:(tt + 1) * 32],
                                                              in_=ps[:, bank, :].rearrange("p (c j) -> p j c", j=T1)),
                      [psk[bank]], [ussm.k[(n - 4) * 4 + b] for b in range(4)])
            if tt == 1:
                V(lambda e: e.tensor_copy(out=upool.ap[:, :, 0:16], in_=upool0.ap[:, :, 512:528]), upool0.k, upool.k)
            if tt >= 1:
                pool_tile(l, tt, upool, 4)
                if tt < TT - 1:
                    V(lambda e: e.tensor_copy(out=ps_s.ap[:, 0, 0:64].rearrange("p (g t) -> p g t", g=4), in_=upool.ap[:, :, 512:528]),
                      upool.k, [ps_s.k[0]])
                    V(lambda e: e.tensor_copy(out=upool.ap[:, :, 0:16], in_=ps_s.ap[:, 0, 0:64].rearrange("p (g t) -> p g t", g=4)),
                      [ps_s.k[0]], upool.k)
        for kt in range(DT):
            dma("pool", w_out_b.ap[:, kt, :], w_out[l, kt * 128:(kt + 1) * 128, :], [], [w_out_b.k[kt]], "w_out")
        if l == 0:
            dbg("ussm", ussm)
        for t4 in range(4):
            ssm_group([t4 * 4 + q for q in range(4)], False)
        if l == 0:
            dbg("Xe1", Xe)
        level2()
        if l == 0:
            dbg("Xe1b", Xe)
        level3(zeros.ap[:, 0, 0:NPAIR], zeros.ap[:, 0, 0:NPAIR], zeros.k)
        V(lambda e: e.tensor_copy(out=cblk.ap[:, 0:32].rearrange("p (a b) -> p a b", a=2), in_=Fb.ap[:, :, :, 8]), Fb.k, cblk.k)
        V(lambda e: e.tensor_copy(out=cblk.ap[:, 32:96].rearrange("p (g t) -> p g t", g=4), in_=upool.ap[:, :, 512:528]),
          upool.k, cblk.k)
        dma("sp", carry_out[l], cblk.ap, cblk.k, [dkey("carry_out")], "carry_o")
        if use_ag:
            dma("pool", cc_src[l].ap(), cblk.ap, cblk.k, [dkey("cc_src%d" % l)], "cc_s%d" % l)
            P.op("pool", lambda e: e.collective_compute("AllGather", ALU.bypass, replica_groups=[list(range(8))],
                                                        ins=[cc_src[l].ap().opt()], outs=[cc_dst[l].ap().opt()]),
                 [dkey("cc_src%d" % l)], [dkey("cc_dst%d" % l)], dma="ccs%d" % l, inc=1)
            dma("pool", gath.ap, cc_dst[l].ap().rearrange("(r p) w -> p r w", p=128), [dkey("cc_dst%d" % l)], gath.k, "cc_g%d" % l)
            V(lambda e: e.tensor_scalar(out=cblk.ap, in0=gath.ap[:, 0, :], scalar1=selb.ap[:, 0:1], scalar2=None, op0=ALU.mult),
              gath.k + selb.k, cblk.k)
            for r in range(1, 8):
                V(lambda e, r=r: e.scalar_tensor_tensor(out=cblk.ap, in0=gath.ap[:, r, :], scalar=selb.ap[:, r:r + 1], in1=cblk.ap,
                                                        op0=ALU.mult, op1=ALU.add),
                  gath.k + selb.k + cblk.k, cblk.k)
        else:
            dma("sp", cblk.ap, carry_in[l], cblk.k, cblk.k, "carry_i")
        V(lambda e: e.tensor_copy(out=sin_b.ap, in_=cblk.ap[:, 0:32].rearrange("p (a b) -> p a b", a=2)), cblk.k, sin_b.k)
        V(lambda e: e.tensor_copy(out=upool0.ap[:, :, 0:16], in_=cblk.ap[:, 32:96].rearrange("p (g t) -> p g t", g=4)),
          cblk.k, upool0.k)
        level3(sin_b.ap[:, 0], sin_b.ap[:, 1], sin_b.k)
        propagate()
        if l == 0:
            dbg("Xe2", Xe)
            dbg("Fb", Fb)
        pool_tile(l, 0, upool0, 2)
        for t4 in range(4):
            ssm_final_tile(t4)
            gelu_tile(t4)
        for blk in range(4):
            for n in range(4):
                bank = n
                for kt in range(4):
                    T(lambda e, n=n, kt=kt, blk=blk, bank=bank: e.matmul(
                        ps[:, bank, :], lhsT=w_glu_b.ap[:, kt, n * 128:(n + 1) * 128],
                        rhs=ussm.ap[:, kt, 4 * blk:4 * blk + 4, :].rearrange("p a b -> p (a b)"),
                        start=(kt == 0), stop=(kt == 3)),
                      [w_glu_b.k[0], ussm.k[kt * 4 + blk]], [psk[bank]])
            for n in range(4):
                s = n % 2
                A(lambda e, n=n, s=s: e.activation(out=gt.ap[:, 0, s, :], in_=ps[:, n, :], func=AF.Sigmoid, bias=bglu.ap[:, l, n:n + 1]),
                  [psk[n], bglu.k[0]], [gt.k[s]])
                yv = ussm.ap[:, n, 4 * blk:4 * blk + 4, :].rearrange("p a b -> p (a b)")
                V(lambda e, yv=yv, s=s: e.tensor_mul(out=yv, in0=yv, in1=gt.ap[:, 0, s, :]),
                  [gt.k[s], ussm.k[n * 4 + blk]], [ussm.k[n * 4 + blk]])
        for blk in range(4):
            for n in range(DT):
                bank = (blk * DT + n) % 8
                for kt in range(DT):
                    src = ymp if kt < 4 else ussm
                    T(lambda e, n=n, kt=kt, blk=blk, bank=bank, src=src: e.matmul(
                        ps[:, bank, :], lhsT=w_out_b.ap[:, kt, n * 128:(n + 1) * 128],
                        rhs=src.ap[:, kt % 4, 4 * blk:4 * blk + 4, :].rearrange("p a b -> p (a b)"),
                        start=(kt == 0), stop=(kt == DT - 1)),
                      [w_out_b.k[kt], src.k[(kt % 4) * 4 + blk]], [psk[bank]])
                hv = h.ap[:, n, :].rearrange("p (c j) -> p j c", j=T1)[:, 4 * blk:4 * blk + 4, :]
                V(lambda e, hv=hv, bank=bank: e.tensor_add(out=hv, in0=hv, in1=ps[:, bank, :].rearrange("p (a b) -> p a b", a=4)),
                  [psk[bank]] + [hk(n, t) for t in range(TT)], [hk(n, t) for t in range(TT)])
        for sub in range(2):
            for tl in range(2):
                tt = sub * 2 + tl
                norm_tile(tt, g2.ap[:, l, :], sq2, rstd2.ap, rstd2.k, lambda dt, tl=tl: xn2.ap[:, dt, tl * 512:(tl + 1) * 512],
                          [xn2.k[dt * 2 + tl] for dt in range(DT)], 7)
            for f in range(FT):
                slot = f % 2
                for gi, wsrc in enumerate((w_gate, w_up)):
                    dma("pool", wgu.ap[:, slot, gi], wsrc[l, :, f * 128:(f + 1) * 128].rearrange("(kt p) n -> p kt n", p=128),
                        [], [wgu.k[slot]], "wgu%d" % slot)
                for tl in range(2):
                    bg = (f % 2) * 4 + tl * 2
                    for gi in range(2):
                        for kt in range(DT):
                            T(lambda e, gi=gi, kt=kt, tl=tl, slot=slot, bg=bg: e.matmul(
                                ps[:, bg + gi, :], lhsT=wgu.ap[:, slot, gi, kt, :], rhs=xn2.ap[:, kt, tl * 512:(tl + 1) * 512],
                                start=(kt == 0), stop=(kt == DT - 1)),
                              [wgu.k[slot], xn2.k[kt * 2 + tl]], [psk[bg + gi]])
                    s = tl
                    A(lambda e, bg=bg, s=s: e.activation(out=sg.ap[:, s, :], in_=ps[:, bg, :], func=AF.Silu), [psk[bg]], [sg.k[s]])
                    V(lambda e, bg=bg, s=s, f=f, tl=tl: e.tensor_mul(out=act.ap[:, f, tl * 512:(tl + 1) * 512], in0=sg.ap[:, s, :],
                                                                     in1=ps[:, bg + 1, :]),
                      [sg.k[s], psk[bg + 1]], [act.k[f * 2 + tl]])
            for n in range(DT):
                slot = n % 2
                dma("pool", wd.ap[:, slot], w_down[l, :, n * 128:(n + 1) * 128].rearrange("(kt p) n -> p kt n", p=128),
                    [], [wd.k[slot]], "wd%d" % slot)
                for tl in range(2):
                    tt = sub * 2 + tl
                    bank = (n * 2 + tl) % 8
                    for f in range(FT):
                        T(lambda e, f=f, tl=tl, slot=slot, bank=bank: e.matmul(
                            ps[:, bank, :], lhsT=wd.ap[:, slot, f, :], rhs=act.ap[:, f, tl * 512:(tl + 1) * 512],
                            start=(f == 0), stop=(f == FT - 1)),
                          [wd.k[slot], act.k[f * 2 + tl]], [psk[bank]])
                    hv = h.ap[:, n, tt * 512:(tt + 1) * 512]
                    V(lambda e, hv=hv, bank=bank: e.tensor_add(out=hv, in0=hv, in1=ps[:, bank, :]),
                      [psk[bank], hk(n, tt)], [hk(n, tt)])

    for l in range(nl):
        layer(l, l == nl - 1)

    finals = []
    for dt in range(DT):
        finals.append(dma("sp", hT_out[dt * 128:(dt + 1) * 128, :], h.ap[:, dt, :], [hk(dt, t) for t in range(TT)],
                          [dkey("hT_out")], "out_h"))
    for tt in range(TT):
        for dt in range(DT):
            s = dt % 2
            A(lambda e, dt=dt, s=s, tt=tt: e.activation(out=sq2.ap[:, s, :], in_=h.ap[:, dt, tt * 512:(tt + 1) * 512], func=AF.Square),
              [hk(dt, tt)], [sq2.k[s]])
            T(lambda e, dt=dt, s=s: e.matmul(ps[:, 7, :], lhsT=ones_bf.ap, rhs=sq2.ap[:, s, :], start=(dt == 0), stop=(dt == DT - 1)),
              [sq2.k[s], ones_bf.k[0]], [psk[7]])
        V(lambda e: e.tensor_scalar(out=rstd2.ap, in0=ps[:, 7, :], scalar1=1.0 / D, scalar2=EPS, op0=ALU.mult, op1=ALU.add),
          [psk[7]], rstd2.k)
        A(lambda e: e.activation(out=rstd2.ap, in_=rstd2.ap, func=AF.Sqrt), rstd2.k, rstd2.k)
        V(lambda e: e.reciprocal(out=rstd2.ap, in_=rstd2.ap), rstd2.k, rstd2.k)
        for dt in range(DT):
            s = dt % 2
            V(lambda e, dt=dt, s=s, tt=tt: e.scalar_tensor_tensor(out=sg.ap[:, s, :], in0=h.ap[:, dt, tt * 512:(tt + 1) * 512],
                                                                  scalar=gf.ap[:, dt:dt + 1], in1=rstd2.ap, op0=ALU.mult, op1=ALU.mult),
              [hk(dt, tt), rstd2.k[0], gf.k[0]], [sg.k[s]])
            finals.append(dma("sp", yT[dt * 128:(dt + 1) * 128, tt * 512:(tt + 1) * 512], sg.ap[:, s, :], [sg.k[s]],
                              [dkey("yT")], "out_y%d" % s))
    fw = {}
    for d in finals + dbg_finals + [dk["carry_out"].last_w]:
        fw[d[1]] = max(fw.get(d[1], 0), d[2])
    P.emit([("dma", n, v) for n, v in fw.items()])
    nc._names = P.names
    return nc


_CACHE = {}


def _consts():
    c = np.zeros((128, 512), np.float32)
    c[:, 0:128] = np.eye(128, dtype=np.float32)
    rows = np.arange(128)
    gi = (rows // 16) % 2
    c[:, 128] = (gi == 0)
    c[:, 129] = (gi == 1)
    c[:, 130:258] = 1.0
    return c


def _cnt_tab(first_half):
    t = np.zeros((128, 4, 16), np.float32)
    for g, w in enumerate((2, 4, 8, 16)):
        for i in range(16):
            t[:, g, i] = 1.0 / (min(i + 1, w) if first_half else w)
    return t.reshape(128, 64)


WNAMES = ["norm_mix", "w_in", "w_pool", "pool_scale", "lam_re", "lam_im", "log_dt", "b_re", "b_im", "c_re", "c_im",
          "d_skip", "w_glu", "b_glu", "w_out", "norm_ffn", "w_gate", "w_up", "w_down"]


MODE = "fused"


def _common_maps(x, nf):
    consts = _consts()
    maps = []
    for r in range(8):
        b, half = r // 2, r % 2
        m = {}
        m["xT"] = np.ascontiguousarray(x[b, half * NTOK:(half + 1) * NTOK, :].T)
        m["norm_final"] = nf
        m["consts"] = consts
        m["cnt_tab"] = _cnt_tab(half == 0)
        s = np.zeros((128, 8), np.float32)
        if half == 1:
            s[:, r - 1] = 1.0
        m["sel"] = s
        maps.append(m)
    return maps


def kernel(**inputs):
    x = np.asarray(inputs["x"], np.float32)
    W = {k: np.ascontiguousarray(np.asarray(inputs[k], np.float32)) for k in WNAMES}
    nf = np.ascontiguousarray(np.asarray(inputs["norm_final"], np.float32))
    out = np.empty((4, 4096, D), np.float32)
    base = _common_maps(x, nf)
    if MODE == "fused":
        key = ("fused",)
        if key not in _CACHE:
            _CACHE[key] = build(DEPTH, True)
        nc = _CACHE[key]
        in_maps = []
        for r in range(8):
            m = dict(base[r])
            m.update(W)
            m["carry_in"] = np.zeros((DEPTH, 128, CARRY_W), np.float32)
            in_maps.append(m)
        res = run_bass_kernel_spmd(nc, in_maps, core_ids=list(range(8)))
        for r in range(8):
            b, half = r // 2, r % 2
            out[b, half * NTOK:(half + 1) * NTOK, :] = np.asarray(res.results[r]["yT"]).T
        return out
    key = ("layer",)
    if key not in _CACHE:
        _CACHE[key] = build(1, False)
    nc = _CACHE[key]
    Wl = [{k: np.ascontiguousarray(W[k][l:l + 1]) for k in WNAMES} for l in range(DEPTH)]
    hT = [base[r]["xT"] for r in range(8)]
    carry = [np.zeros((1, 128, CARRY_W), np.float32) for _ in range(8)]
    for step in range(DEPTH + 1):
        in_maps = []
        lay = []
        for r in range(8):
            half = r % 2
            l = step if half == 0 else step - 1
            lay.append(l)
            lw = min(max(l, 0), DEPTH - 1)
            m = dict(base[r])
            m.update(Wl[lw])
            m["xT"] = hT[r]
            m["carry_in"] = carry[r]
            in_maps.append(m)
        res = run_bass_kernel_spmd(nc, in_maps, core_ids=list(range(8)))
        for r in range(8):
            l = lay[r]
            if l < 0 or l >= DEPTH:
                continue
            hT[r] = np.ascontiguousarray(np.asarray(res.results[r]["hT_out"], np.float32))
            if r % 2 == 0:
                carry[r + 1] = np.ascontiguousarray(np.asarray(res.results[r]["carry_out"], np.float32))
            if l == DEPTH - 1:
                b, half = r // 2, r % 2
                out[b, half * NTOK:(half + 1) * NTOK, :] = np.asarray(res.results[r]["yT"]).T
    return out
```
